# Optimizing a Trainium2 kernel written in Bass

```python
import jax, jax.numpy as jnp
from jax import lax
import numpy as np

D_MODEL = 4096
BATCH = 4
SEQ = 2048
DEPTH = 2
DEC_BATCH = 128
DEC_SEQ = 1
PAST_LEN = 16384
PAGE_SIZE = 128

N_EVEN = (DEPTH + 1) // 2
N_ODD = DEPTH // 2
MIX_WIDTH = D_MODEL
CONV_WIDTH = 4
GDN_HEAD_DIM = 128
GDN_WIDTH = MIX_WIDTH // 2
GDN_HEADS = GDN_WIDTH // GDN_HEAD_DIM
GDN_CHUNK = 64
LRU_WIDTH = MIX_WIDTH - GDN_WIDTH
LRU_BLOCKS = 16
LRU_BLOCK_DIM = LRU_WIDTH // LRU_BLOCKS
LRU_C = 8.0
HGRN_EXPAND = 128
HGRN_HEADS = MIX_WIDTH // HGRN_EXPAND
HGRN_KEY_DIM = HGRN_EXPAND
HGRN_VAL_DIM = MIX_WIDTH // HGRN_HEADS
HGRN_KEY_WIDTH = HGRN_HEADS * HGRN_KEY_DIM
HGRN_CHUNK = 32
MEM_TOKENS = 256
MEM_HEADS = 4
MEM_HEAD_DIM = 128
MEM_WIDTH = MEM_HEADS * MEM_HEAD_DIM
D_FF = 4 * D_MODEL
EPS = 1e-6
AB_IN = 4 * GDN_WIDTH + 2 * GDN_HEADS + 2 * LRU_WIDTH
AB_SPLITS = (3 * GDN_WIDTH, 4 * GDN_WIDTH, 4 * GDN_WIDTH + GDN_HEADS, 4 * GDN_WIDTH + 2 * GDN_HEADS,
             4 * GDN_WIDTH + 2 * GDN_HEADS + LRU_WIDTH)
C_IN = 2 * HGRN_KEY_WIDTH + 2 * MIX_WIDTH
C_SPLITS = (HGRN_KEY_WIDTH, 2 * HGRN_KEY_WIDTH, 2 * HGRN_KEY_WIDTH + MIX_WIDTH)

kernel_name = 'hybrid_gdn_rglru_hgrn2_memory_decoder_step'


def rms_norm(x, w):
    xf = x.astype(jnp.float32)
    y = xf * lax.rsqrt(jnp.mean(xf * xf, axis=-1, keepdims=True) + EPS)
    return (y * w.astype(jnp.float32)).astype(x.dtype)


def l2_normalize(x):
    return x * lax.rsqrt(jnp.sum(x * x, axis=-1, keepdims=True) + EPS)


def causal_conv(x, prev, w, b=None):
    L = x.shape[1]
    xp = jnp.concatenate([prev.astype(x.dtype), x], axis=1)
    y = xp[:, 0:L] * w[0]
    for tap in range(1, CONV_WIDTH):
        y = y + xp[:, tap:tap + L] * w[tap]
    if b is not None:
        y = y + b
    return y, xp[:, L:]


def _to_chunks(t, c):
    b, L = t.shape[:2]
    t = t.reshape(b, L // c, c, *t.shape[2:])
    return jnp.moveaxis(jnp.moveaxis(t, 1, 0), 2, 3)


def _from_chunks(t):
    n, b, h, c = t.shape[:4]
    t = jnp.moveaxis(jnp.moveaxis(t, 3, 2), 0, 1)
    return t.reshape(b, n * c, h, *t.shape[4:])


def gdn_chunked(q, k, v, g, beta, s0):
    C = GDN_CHUNK
    q, k, v = _to_chunks(q, C), _to_chunks(k, C), _to_chunks(v, C)
    g, beta = _to_chunks(g, C), _to_chunks(beta, C)
    gc = jnp.cumsum(g, axis=-1)
    causal = jnp.tril(jnp.ones((C, C), dtype=bool))
    strict = jnp.tril(jnp.ones((C, C), dtype=bool), k=-1)
    decay = jnp.exp(jnp.where(causal, gc[..., :, None] - gc[..., None, :], -jnp.inf))
    k_beta = k * beta[..., None]
    m = jnp.where(strict, jnp.einsum('nbhik,nbhjk->nbhij', k_beta, k) * decay, 0.0)
    eye = jnp.eye(C, dtype=m.dtype)
    dv = v.shape[-1]
    rhs = jnp.concatenate([v * beta[..., None], k_beta * jnp.exp(gc)[..., None]], axis=-1)
    sol = lax.linalg.triangular_solve(m + eye, rhs, left_side=True, lower=True, unit_diagonal=True)
    u, w = sol[..., :dv], sol[..., dv:]
    attn = jnp.einsum('nbhik,nbhjk->nbhij', q, k) * decay
    q_dec = q * jnp.exp(gc)[..., None]
    k_dec = k * jnp.exp(gc[..., -1:] - gc)[..., None]
    g_last = jnp.exp(gc[..., -1])

    def step(s, inp):
        u_n, w_n, a_n, qd_n, kd_n, gl_n = inp
        v_new = u_n - jnp.einsum('bhck,bhkv->bhcv', w_n, s)
        o_n = jnp.einsum('bhck,bhkv->bhcv', qd_n, s) + jnp.einsum('bhij,bhjv->bhiv', a_n, v_new)
        s = s * gl_n[..., None, None] + jnp.einsum('bhck,bhcv->bhkv', kd_n, v_new)
        return s, o_n

    s, o = lax.scan(step, s0, (u, w, attn, q_dec, k_dec, g_last))
    return _from_chunks(o), s


def gdn_recurrent(q, k, v, g, beta, s0):
    def step(s, inp):
        q_t, k_t, v_t, g_t, b_t = inp
        s = s * jnp.exp(g_t)[..., None, None]
        v_new = b_t[..., None] * (v_t - jnp.einsum('bhk,bhkv->bhv', k_t, s))
        s = s + k_t[..., :, None] * v_new[..., None, :]
        return s, jnp.einsum('bhk,bhkv->bhv', q_t, s)

    xs = (jnp.swapaxes(q, 0, 1), jnp.swapaxes(k, 0, 1), jnp.swapaxes(v, 0, 1),
          jnp.swapaxes(g, 0, 1), jnp.swapaxes(beta, 0, 1))
    s, o = lax.scan(step, s0, xs)
    return jnp.swapaxes(o, 0, 1), s


def hgrn_chunked(q, k, v, log_f, s0):
    C = HGRN_CHUNK
    q, k, v = _to_chunks(q, C), _to_chunks(k, C), _to_chunks(v, C)
    gc = jnp.cumsum(_to_chunks(log_f, C), axis=3)
    causal = jnp.tril(jnp.ones((C, C), dtype=bool))[:, :, None]

    def step(s, inp):
        q_n, k_n, v_n, g_n = inp
        rel = jnp.exp(jnp.where(causal, g_n[:, :, :, None, :] - g_n[:, :, None, :, :], -jnp.inf))
        a_n = jnp.einsum('bhik,bhjk,bhijk->bhij', q_n, k_n, rel)
        o_n = jnp.einsum('bhik,bhkv->bhiv', q_n * jnp.exp(g_n), s) + jnp.einsum('bhij,bhjv->bhiv', a_n, v_n)
        gl = g_n[:, :, -1]
        s = s * jnp.exp(gl)[..., None] + jnp.einsum('bhck,bhcv->bhkv', k_n * jnp.exp(gl[:, :, None] - g_n), v_n)
        return s, o_n

    s, o = lax.scan(step, s0, (q, k, v, gc))
    return _from_chunks(o), s


def hgrn_recurrent(q, k, v, log_f, s0):
    def step(s, inp):
        q_t, k_t, v_t, lf_t = inp
        s = s * jnp.exp(lf_t)[..., None] + k_t[..., :, None] * v_t[..., None, :]
        return s, jnp.einsum('bhk,bhkv->bhv', q_t, s)

    xs = (jnp.swapaxes(q, 0, 1), jnp.swapaxes(k, 0, 1), jnp.swapaxes(v, 0, 1), jnp.swapaxes(log_f, 0, 1))
    s, o = lax.scan(step, s0, xs)
    return jnp.swapaxes(o, 0, 1), s


def _linear_combine(earlier, later):
    a1, b1 = earlier
    a2, b2 = later
    return a1 * a2, a2 * b1 + b2


def rg_lru(x, h0, w_a, b_a, w_i, b_i, lam, reset):
    bsz, L, W = x.shape
    xb = x.reshape(bsz, L, LRU_BLOCKS, LRU_BLOCK_DIM)
    gate_a = jax.nn.sigmoid(jnp.einsum('blni,nij->blnj', xb, w_a).reshape(bsz, L, W) + b_a)
    gate_i = jax.nn.sigmoid(jnp.einsum('blni,nij->blnj', xb, w_i).reshape(bsz, L, W) + b_i)
    log_a = -LRU_C * gate_a * jax.nn.softplus(-lam)
    a = jnp.exp(log_a)
    mult = jnp.where(reset[None, :, None], 1.0, jnp.sqrt(-jnp.expm1(2.0 * log_a)))
    b = mult * gate_i * x
    b = b.at[:, 0].add(a[:, 0] * h0)
    _, h = lax.associative_scan(_linear_combine, (a, b), axis=1)
    return h, h[:, -1]


def ab_mixer(h, w_in, w_out, conv_w, a_log, dt_bias, norm_w, lconv_w, lconv_b, w_a, b_a, w_i, b_i, lam,
             conv_prev, s0, lconv_prev, h0, reset, prompt):
    bsz, L, _ = h.shape
    f32 = jnp.float32
    qkv, z, b_raw, a_raw, xl, yl = jnp.split(h @ w_in, AB_SPLITS, axis=-1)
    qkv, conv_new = causal_conv(qkv, conv_prev, conv_w)
    qkv = jax.nn.silu(qkv).astype(f32).reshape(bsz, L, 3 * GDN_HEADS, GDN_HEAD_DIM)
    q, k, v = jnp.split(qkv, 3, axis=2)
    q = l2_normalize(q) * GDN_HEAD_DIM ** -0.5
    k = l2_normalize(k)
    beta = jax.nn.sigmoid(b_raw.astype(f32))
    g = -jnp.exp(a_log.astype(f32)) * jax.nn.softplus(a_raw.astype(f32) + dt_bias.astype(f32))
    gdn_fn = gdn_chunked if prompt else gdn_recurrent
    o, s_new = gdn_fn(q, k, v, g, beta, s0.astype(f32))
    gate = jax.nn.silu(z.astype(f32)).reshape(bsz, L, GDN_HEADS, GDN_HEAD_DIM)
    o_a = (rms_norm(o, norm_w) * gate).reshape(bsz, L, GDN_WIDTH)
    xl, lconv_new = causal_conv(xl, lconv_prev, lconv_w, lconv_b)
    hl, h_new = rg_lru(xl.astype(f32), h0.astype(f32), w_a, b_a, w_i, b_i, lam.astype(f32), reset)
    o_b = hl * jax.nn.gelu(yl.astype(f32), approximate=True)
    out = jnp.concatenate([o_a, o_b], axis=-1).astype(h.dtype) @ w_out
    return out, conv_new, s_new, lconv_new, h_new


def c_mixer(h, w_in, w_out, lower_bound, norm_w, s0, prompt):
    bsz, L, _ = h.shape
    f32 = jnp.float32
    q, f_raw, i, gz = jnp.split(h @ w_in, C_SPLITS, axis=-1)
    kshape = (bsz, L, HGRN_HEADS, HGRN_KEY_DIM)
    vshape = (bsz, L, HGRN_HEADS, HGRN_VAL_DIM)
    q = jax.nn.silu(q.astype(f32)).reshape(kshape) * HGRN_KEY_DIM ** -0.5
    lb = lower_bound.astype(f32)
    f = (lb + (1.0 - lb) * jax.nn.sigmoid(f_raw.astype(f32))).reshape(kshape)
    log_f = jnp.log(f)
    k = 1.0 - f
    v = i.astype(f32).reshape(vshape)
    hgrn_fn = hgrn_chunked if prompt else hgrn_recurrent
    o, s_new = hgrn_fn(q, k, v, log_f, s0.astype(f32))
    o = rms_norm(o, norm_w) * jax.nn.silu(gz.astype(f32)).reshape(vshape)
    return o.reshape(bsz, L, MIX_WIDTH).astype(h.dtype) @ w_out, s_new


def mem_project(mem, norm_w, w_k, w_v):
    m = rms_norm(mem, norm_w)
    shp = (mem.shape[0], mem.shape[1], MEM_HEADS, MEM_HEAD_DIM)
    return (m @ w_k).reshape(shp), (m @ w_v).reshape(shp)


def mem_attend(h, mk, mv, w_q, w_o):
    bsz, L, _ = h.shape
    q = (h @ w_q).reshape(bsz, L, MEM_HEADS, MEM_HEAD_DIM)
    s = jnp.einsum('blhd,bmhd->bhlm', q, mk.astype(q.dtype)).astype(jnp.float32) * MEM_HEAD_DIM ** -0.5
    p = jax.nn.softmax(s, axis=-1).astype(q.dtype)
    o = jnp.einsum('bhlm,bmhd->blhd', p, mv.astype(q.dtype)).reshape(bsz, L, MEM_WIDTH)
    return o @ w_o


def trunk(x, mem_k, mem_v, gdn_conv, gdn_state, lru_conv, lru_state, hgrn_state, reset, prompt, params):
    (norm_mix, norm_mem, norm_ffn, norm_final, ab_w_in, ab_w_out, gdn_conv_w, gdn_a_log, gdn_dt_bias,
     gdn_norm_w, lru_conv_w, lru_conv_b, lru_w_a, lru_b_a, lru_w_i, lru_b_i, lru_lam, c_w_in, c_w_out,
     hgrn_lb_raw, hgrn_norm_w, mem_w_q, mem_w_o, ffn_w_up, ffn_w_down) = params
    lb_w = jax.nn.softmax(hgrn_lb_raw.astype(jnp.float32), axis=0)
    lower_bounds = jnp.cumsum(lb_w, axis=0) - lb_w[0]
    n_gdn_conv, n_gdn, n_lru_conv, n_lru, n_hgrn = [], [], [], [], []
    for l in range(DEPTH):
        h = rms_norm(x, norm_mix[l])
        if l % 2 == 0:
            e = l // 2
            out, c1, s1, c2, s2 = ab_mixer(
                h, ab_w_in[e], ab_w_out[e], gdn_conv_w[e], gdn_a_log[e], gdn_dt_bias[e], gdn_norm_w[e],
                lru_conv_w[e], lru_conv_b[e], lru_w_a[e], lru_b_a[e], lru_w_i[e], lru_b_i[e], lru_lam[e],
                gdn_conv[e], gdn_state[e], lru_conv[e], lru_state[e], reset, prompt)
            n_gdn_conv.append(c1)
            n_gdn.append(s1)
            n_lru_conv.append(c2)
            n_lru.append(s2)
        else:
            o = l // 2
            out, s3 = c_mixer(h, c_w_in[o], c_w_out[o], lower_bounds[l], hgrn_norm_w[o], hgrn_state[o], prompt)
            n_hgrn.append(s3)
        x = x + out
        x = x + mem_attend(rms_norm(x, norm_mem[l]), mem_k[l], mem_v[l], mem_w_q[l], mem_w_o[l])
        hf = rms_norm(x, norm_ffn[l])
        x = x + jnp.square(jax.nn.relu(hf @ ffn_w_up[l])) @ ffn_w_down[l]
    y = rms_norm(x, norm_final)
    return (y, jnp.stack(n_gdn_conv), jnp.stack(n_gdn), jnp.stack(n_lru_conv), jnp.stack(n_lru),
            jnp.stack(n_hgrn))


def _normal(k, shape, scale):
    return jax.random.normal(k, shape, jnp.float32) * scale


def setup_inputs(seed: int = 0) -> dict:
    key = jax.random.key(seed)
    ks = jax.random.split(key, 40)
    f32 = jnp.float32
    gain = lambda k, shape: 1.0 + _normal(k, shape, 0.02)
    dt = jnp.exp(jax.random.uniform(ks[19], (N_EVEN, GDN_HEADS), f32, np.log(1e-3), np.log(1e-1)))
    a0 = jax.random.uniform(ks[27], (N_EVEN, LRU_WIDTH), f32, 0.9, 0.999)
    sig = a0 ** (1.0 / LRU_C)
    return {
        'x_prompt': _normal(ks[0], (BATCH, SEQ, D_MODEL), 1.0),
        'x_sample': _normal(ks[1], (DEC_BATCH, DEC_SEQ, D_MODEL), 1.0),
        'cache_mem_k': _normal(ks[2], (DEPTH, DEC_BATCH, MEM_TOKENS, MEM_HEADS, MEM_HEAD_DIM), 1.0),
        'cache_mem_v': _normal(ks[3], (DEPTH, DEC_BATCH, MEM_TOKENS, MEM_HEADS, MEM_HEAD_DIM), 1.0),
        'state_gdn_conv': _normal(ks[4], (N_EVEN, DEC_BATCH, CONV_WIDTH - 1, 3 * GDN_WIDTH), 1.0),
        'state_gdn': _normal(ks[5], (N_EVEN, DEC_BATCH, GDN_HEADS, GDN_HEAD_DIM, GDN_HEAD_DIM), 0.1),
        'state_lru_conv': _normal(ks[6], (N_EVEN, DEC_BATCH, CONV_WIDTH - 1, LRU_WIDTH), 1.0),
        'state_lru': _normal(ks[7], (N_EVEN, DEC_BATCH, LRU_WIDTH), 0.5),
        'state_hgrn': _normal(ks[8], (N_ODD, DEC_BATCH, HGRN_HEADS, HGRN_KEY_DIM, HGRN_VAL_DIM), 0.5),
        'mem_prompt': _normal(ks[9], (BATCH, MEM_TOKENS, D_MODEL), 1.0),
        'norm_mix': gain(ks[10], (DEPTH, D_MODEL)),
        'norm_mem': gain(ks[11], (DEPTH, D_MODEL)),
        'norm_mem_kv': gain(ks[12], (DEPTH, D_MODEL)),
        'norm_ffn': gain(ks[13], (DEPTH, D_MODEL)),
        'norm_final': gain(ks[14], (D_MODEL,)),
        'ab_w_in': _normal(ks[15], (N_EVEN, D_MODEL, AB_IN), D_MODEL ** -0.5),
        'ab_w_out': _normal(ks[16], (N_EVEN, MIX_WIDTH, D_MODEL), MIX_WIDTH ** -0.5),
        'gdn_conv_w': _normal(ks[17], (N_EVEN, CONV_WIDTH, 3 * GDN_WIDTH), CONV_WIDTH ** -0.5),
        'gdn_a_log': jnp.log(jax.random.uniform(ks[18], (N_EVEN, GDN_HEADS), f32, 1.0, 16.0)),
        'gdn_dt_bias': dt + jnp.log(-jnp.expm1(-dt)),
        'gdn_norm_w': gain(ks[20], (N_EVEN, GDN_HEAD_DIM)),
        'lru_conv_w': _normal(ks[21], (N_EVEN, CONV_WIDTH, LRU_WIDTH), CONV_WIDTH ** -0.5),
        'lru_conv_b': _normal(ks[22], (N_EVEN, LRU_WIDTH), 0.01),
        'lru_w_a': _normal(ks[23], (N_EVEN, LRU_BLOCKS, LRU_BLOCK_DIM, LRU_BLOCK_DIM), LRU_BLOCK_DIM ** -0.5),
        'lru_b_a': _normal(ks[24], (N_EVEN, LRU_WIDTH), 0.01),
        'lru_w_i': _normal(ks[25], (N_EVEN, LRU_BLOCKS, LRU_BLOCK_DIM, LRU_BLOCK_DIM), LRU_BLOCK_DIM ** -0.5),
        'lru_b_i': _normal(ks[26], (N_EVEN, LRU_WIDTH), 0.01),
        'lru_lam': jnp.log(sig) - jnp.log1p(-sig),
        'c_w_in': _normal(ks[28], (N_ODD, D_MODEL, C_IN), D_MODEL ** -0.5),
        'c_w_out': _normal(ks[29], (N_ODD, MIX_WIDTH, D_MODEL), MIX_WIDTH ** -0.5),
        'hgrn_lb_raw': gain(ks[30], (DEPTH, HGRN_KEY_WIDTH)),
        'hgrn_norm_w': gain(ks[31], (N_ODD, HGRN_VAL_DIM)),
        'mem_w_q': _normal(ks[32], (DEPTH, D_MODEL, MEM_WIDTH), D_MODEL ** -0.5),
        'mem_w_k': _normal(ks[33], (DEPTH, D_MODEL, MEM_WIDTH), D_MODEL ** -0.5),
        'mem_w_v': _normal(ks[34], (DEPTH, D_MODEL, MEM_WIDTH), D_MODEL ** -0.5),
        'mem_w_o': _normal(ks[35], (DEPTH, MEM_WIDTH, D_MODEL), MEM_WIDTH ** -0.5),
        'ffn_w_up': _normal(ks[36], (DEPTH, D_MODEL, D_FF), D_MODEL ** -0.5),
        'ffn_w_down': _normal(ks[37], (DEPTH, D_FF, D_MODEL), D_FF ** -0.5),
    }


def reference(x_prompt, x_sample, cache_mem_k, cache_mem_v, state_gdn_conv, state_gdn, state_lru_conv,
              state_lru, state_hgrn, mem_prompt, norm_mix, norm_mem, norm_mem_kv, norm_ffn, norm_final,
              ab_w_in, ab_w_out, gdn_conv_w, gdn_a_log, gdn_dt_bias, gdn_norm_w, lru_conv_w, lru_conv_b,
              lru_w_a, lru_b_a, lru_w_i, lru_b_i, lru_lam, c_w_in, c_w_out, hgrn_lb_raw, hgrn_norm_w,
              mem_w_q, mem_w_k, mem_w_v, mem_w_o, ffn_w_up, ffn_w_down):
    params = (norm_mix, norm_mem, norm_ffn, norm_final, ab_w_in, ab_w_out, gdn_conv_w, gdn_a_log, gdn_dt_bias,
              gdn_norm_w, lru_conv_w, lru_conv_b, lru_w_a, lru_b_a, lru_w_i, lru_b_i, lru_lam, c_w_in, c_w_out,
              hgrn_lb_raw, hgrn_norm_w, mem_w_q, mem_w_o, ffn_w_up, ffn_w_down)
    bp, lp = x_prompt.shape[0], x_prompt.shape[1]
    adt = x_prompt.dtype
    kv_pairs = [mem_project(mem_prompt, norm_mem_kv[l], mem_w_k[l], mem_w_v[l]) for l in range(DEPTH)]
    p_mem_k = jnp.stack([kv[0] for kv in kv_pairs])
    p_mem_v = jnp.stack([kv[1] for kv in kv_pairs])
    z_gdn_conv = jnp.zeros((N_EVEN, bp, CONV_WIDTH - 1, 3 * GDN_WIDTH), adt)
    z_gdn = jnp.zeros((N_EVEN, bp, GDN_HEADS, GDN_HEAD_DIM, GDN_HEAD_DIM), jnp.float32)
    z_lru_conv = jnp.zeros((N_EVEN, bp, CONV_WIDTH - 1, LRU_WIDTH), adt)
    z_lru = jnp.zeros((N_EVEN, bp, LRU_WIDTH), jnp.float32)
    z_hgrn = jnp.zeros((N_ODD, bp, HGRN_HEADS, HGRN_KEY_DIM, HGRN_VAL_DIM), jnp.float32)
    reset_p = jnp.arange(lp) == 0
    y_prompt, p_gdn_conv, p_gdn, p_lru_conv, p_lru, p_hgrn = trunk(
        x_prompt, p_mem_k, p_mem_v, z_gdn_conv, z_gdn, z_lru_conv, z_lru, z_hgrn, reset_p, True, params)
    reset_s = (PAST_LEN + jnp.arange(x_sample.shape[1])) == 0
    y_sample, s_gdn_conv, s_gdn, s_lru_conv, s_lru, s_hgrn = trunk(
        x_sample, cache_mem_k, cache_mem_v, state_gdn_conv, state_gdn, state_lru_conv, state_lru, state_hgrn,
        reset_s, False, params)
    return (y_prompt, y_sample, p_mem_k, p_mem_v, p_gdn_conv, p_gdn, p_lru_conv, p_lru, p_hgrn,
            s_gdn_conv, s_gdn, s_lru_conv, s_lru, s_hgrn)
```

```python
import numpy as np
import concourse.bass as bass
import concourse.mybir as mybir
from concourse.bass_utils import run_bass_kernel_spmd

F32 = mybir.dt.float32
BF16 = mybir.dt.bfloat16
AF = mybir.ActivationFunctionType
ALU = mybir.AluOpType
EPS = 1e-6
NCORES = 8
TS = 16
NTILE = 256


class Tile:
    def __init__(self, t, name, nsub=1):
        self.t, self.name, self.nsub = t, name, nsub

    def __getitem__(self, idx):
        return self.t[idx]

    def k(self, i):
        return (self.name, i)

    def keys(self):
        return [(self.name, i) for i in range(self.nsub)]


def _keys(lst):
    out = []
    for x in lst:
        if isinstance(x, Tile):
            out.extend(x.keys())
        else:
            out.append(x)
    return out


class Ker:
    def __init__(self):
        self.nc = bass.Bass("TRN2", target_bir_lowering=False)
        self.ops = []
        self.nsl = 0

    def sb(self, name, shape, dt, nsub=1):
        return Tile(self.nc.alloc_sbuf_tensor(name, list(shape), dt), name, nsub)

    def ps(self, name, shape, dt, nsub=1):
        if not hasattr(self, "psn"):
            self.psn = set()
        self.psn.add(name)
        return Tile(self.nc.alloc_psum_tensor(name, list(shape), dt), name, nsub)

    def op(self, eng, meth, R, W, *args, **kw):
        Rk, Wk = _keys(R), _keys(W)
        psn = getattr(self, "psn", ())
        Wk = Wk + [k for k in Rk if k[0] in psn and k not in Wk]
        self.ops.append(dict(eng=eng, meth=meth, R=Rk, W=Wk, args=args, kw=kw, dma=None))

    def dma(self, q, out, in_, R, W, key):
        self.ops.append(dict(eng=q, meth="dma_start", R=_keys(R), W=_keys(W), args=(), kw=dict(out=out, in_=in_),
                             dma=key))

    def act(self, out, in_, func, R, W, **kw):
        self.op("act", "activation", R, W, out=out, in_=in_, func=func, **kw)

    def mm(self, out, lhsT, rhs, R, W, start=True, stop=True):
        self.op("pe", "matmul", R, W, out, lhsT=lhsT, rhs=rhs, start=start, stop=stop)

    def tr(self, out, in_, ident, R, W):
        self.op("pe", "transpose", R, W, out, in_, ident)

    def tt(self, out, in0, in1, op, R, W, eng="dve"):
        self.op(eng, "tensor_tensor", R, W, out=out, in0=in0, in1=in1, op=op)

    def ts(self, out, in0, s1, s2, op0, R, W, op1=None, eng="dve", **kw):
        if op1 is None:
            self.op(eng, "tensor_scalar", R, W, out=out, in0=in0, scalar1=s1, scalar2=None, op0=op0, **kw)
        else:
            self.op(eng, "tensor_scalar", R, W, out=out, in0=in0, scalar1=s1, scalar2=s2, op0=op0, op1=op1, **kw)

    def stt(self, out, in0, scalar, in1, op0, op1, R, W, eng="dve", **kw):
        self.op(eng, "scalar_tensor_tensor", R, W, out=out, in0=in0, scalar=scalar, in1=in1, op0=op0, op1=op1, **kw)

    def cp(self, out, in_, R, W, eng="dve"):
        if eng == "act":
            self.op("act", "activation", R, W, out=out, in_=in_, func=AF.Copy)
        else:
            self.op(eng, "tensor_copy", R, W, out=out, in_=in_)

    def finalize(self):
        nc = self.nc
        engs = {"pe": nc.tensor, "act": nc.scalar, "dve": nc.vector, "pool": nc.gpsimd, "sp": nc.sync}
        ops = self.ops
        lastw = {}
        readers = {}
        dmacnt = {}
        for i, o in enumerate(ops):
            deps = set()
            for r in o["R"]:
                if r in lastw:
                    deps.add(lastw[r])
            for w in o["W"]:
                if w in lastw:
                    deps.add(lastw[w])
                for rr in readers.get(w, ()):
                    deps.add(rr)
            deps.discard(i)
            need = []
            for j in deps:
                pj = ops[j]
                if pj["dma"] is not None:
                    need.append(("dma", pj["dma"], dmacnt[pj["dma"]]))
                elif pj["eng"] == o["eng"] and o["dma"] is None and o["eng"] == "pe":
                    continue
                else:
                    pj["sig"] = True
                    need.append(("eng", pj["eng"], j))
            o["need"] = need
            for w in o["W"]:
                lastw[w] = i
                readers[w] = []
            for r in o["R"]:
                readers.setdefault(r, []).append(i)
            if o["dma"] is not None:
                dmacnt[o["dma"]] = dmacnt.get(o["dma"], 0) + 16
                o["dmaval"] = dmacnt[o["dma"]]
        cnt = {}
        for o in ops:
            if o.get("sig"):
                cnt[o["eng"]] = cnt.get(o["eng"], 0) + 1
                o["sigval"] = cnt[o["eng"]]
        esem = {e: nc.alloc_semaphore("sem_" + e) for e in ("pe", "act", "dve", "pool", "sp")}
        dsem = {}
        waited = {}
        last_dma = {}
        for o in ops:
            e = engs[o["eng"]]
            for kind, a, b in o["need"]:
                if kind == "dma":
                    if a not in dsem:
                        dsem[a] = nc.alloc_semaphore("d_" + a)
                    sem, val, sk = dsem[a], b, ("d", a)
                else:
                    sem, val, sk = esem[a], ops[b]["sigval"], ("e", a)
                wk = (o["eng"], sk)
                if waited.get(wk, 0) >= val:
                    continue
                waited[wk] = val
                e.wait_ge(sem, val)
            ins = getattr(e, o["meth"])(*o["args"], **o["kw"])
            if o["dma"] is not None:
                if o["dma"] not in dsem:
                    dsem[o["dma"]] = nc.alloc_semaphore("d_" + o["dma"])
                ins.then_inc(dsem[o["dma"]], 16)
                last_dma[o["dma"]] = o["dmaval"]
            elif o.get("sig"):
                ins.then_inc(esem[o["eng"]], 1)
        for key, val in last_dma.items():
            nc.sync.wait_ge(dsem[key], val)
        return nc


PV = {}
_c = 0
for _n, _w in [("norm_mix", 64), ("norm_mem", 64), ("norm_mem_kv", 64), ("norm_ffn", 64), ("norm_final", 32),
               ("gdn_conv_w", 192), ("lru_conv_w", 64), ("lru_conv_b", 16), ("lru_b_a", 16), ("lru_b_i", 16),
               ("lru_lam", 16), ("lb_raw", 64), ("gdn_norm_w", 1), ("hgrn_norm_w", 1), ("a_log", 1),
               ("dt_bias", 1)]:
    PV[_n] = _c
    _c += _w
PVN = _c
CI, CO, CUC, CUS, CLS, CHM, CRM, CSEL = 0, 128, 256, 384, 512, 640, 768, 1280
CN = 1280


def make_consts():
    c = np.zeros((128, CN), np.float32)
    p = np.arange(128)[:, None]
    i = np.arange(128)[None, :]
    c[:, CI:CI + 128] = (p == i)
    c[:, CO:CO + 128] = 1.0
    c[:, CUC:CUC + 128] = (p <= i)
    c[:, CUS:CUS + 128] = (p < i)
    c[:, CLS:CLS + 128] = (p > i)
    c[:, CHM:CHM + 128] = (p <= i) & ((p // 64) == (i // 64))
    rm = np.ones(512, np.float32)
    rm[::64] = 0.0
    c[:, CRM:CRM + 512] = rm[None, :]
    return c


def build(NT):
    K = Ker()
    nc = K.nc
    N = NTILE
    L = NT * N

    def din(name, shape):
        return nc.dram_tensor(name, list(shape), F32, kind="ExternalInput").ap()

    def dout(name, shape):
        return nc.dram_tensor(name, list(shape), F32, kind="ExternalOutput").ap()

    xT = din("xT", [4096, L]); xsT = din("xsT", [4096, TS]); memT = din("memT", [4096, 256])
    cmk = din("cmk", [2, TS, 256, 512]); cmv = din("cmv", [2, TS, 256, 512])
    sgc = din("sgc", [6144, 3, TS]); sg = din("sg", [TS, 16, 128, 128])
    slc = din("slc", [2048, 3, TS]); sl = din("sl", [2048, TS]); sh = din("sh", [TS, 32, 128, 128])
    pvd = din("pv", [128, PVN]); cst = din("consts", [128, CN])
    ab_w_in = din("ab_w_in", [4096, 12320]); ab_w_out = din("ab_w_out", [4096, 4096])
    c_w_in = din("c_w_in", [4096, 16384]); c_w_out = din("c_w_out", [4096, 4096])
    mem_w_q = din("mem_w_q", [2, 4096, 512]); mem_w_k = din("mem_w_k", [2, 4096, 512])
    mem_w_v = din("mem_w_v", [2, 4096, 512]); mem_w_o = din("mem_w_o", [2, 512, 4096])
    ffn_w_up = din("ffn_w_up", [2, 4096, 16384]); ffn_w_down = din("ffn_w_down", [2, 16384, 4096])
    lru_w_a = din("lru_w_a", [16, 128, 128]); lru_w_i = din("lru_w_i", [16, 128, 128])

    yT = dout("yT", [4096, L]); ysT = dout("ysT", [4096, TS])
    okT = dout("okT", [2, 512, 256]); ov = dout("ov", [2, 256, 512])
    ogc = dout("ogc", [128, 144]); og = dout("og", [16, 128, 128]); olc = dout("olc", [128, 48])
    ol = dout("ol", [128, 16]); oh = dout("oh", [32, 128, 128])
    osgc = dout("osgc", [6144, 3, TS]); osg = dout("osg", [TS, 16, 128, 128])
    oslc = dout("oslc", [2048, 3, TS]); osl = dout("osl", [128, 16, TS]); osh = dout("osh", [TS, 32, 128, 128])

    xres = K.sb("xres", [128, 32, N], F32, nsub=32)
    hT = K.sb("hT", [128, 32, N], BF16)
    moT = K.sb("moT", [128, 32, N], BF16, nsub=32)
    NSL = 5
    wsl = [K.sb(f"wsl{i}", [128, 4, 512], BF16) for i in range(NSL)]
    pv = K.sb("pvs", [128, PVN], F32)
    cs = K.sb("cs", [128, CN], F32)
    ones_bf = K.sb("ones_bf", [128, 128], BF16)
    ident_bf = K.sb("ident_bf", [128, 128], BF16)
    hm_bf = K.sb("hm_bf", [128, 128], BF16)
    sq = [K.sb(f"sq{i}", [128, 512], BF16) for i in range(2)]
    rstd = K.sb("rstd", [128, 512], F32)
    xp = [K.sb(f"xp{i}", [128, N + 3], F32) for i in range(2)]
    cacc = [K.sb(f"cacc{i}", [128, N], F32) for i in range(2)]
    gA = K.sb("gA", [128, 4, N], BF16, nsub=4)
    gB = K.sb("gB", [128, 4, N], BF16, nsub=4)
    gC = K.sb("gC", [128, 4, N], BF16, nsub=4)
    gD = K.sb("gD", [128, 4, N], BF16, nsub=4)
    gE = K.sb("gE", [128, 4, N], BF16, nsub=4)
    f1 = [K.sb(f"f1_{i}", [128, 512], F32) for i in range(4)]
    convst = K.sb("convst", [128, 48, 3], F32)
    lconvst = K.sb("lconvst", [128, 16, 3], F32)
    hst = K.sb("hst", [128, 16], F32)
    Sg = K.sb("Sg", [128, 16, 128], F32, nsub=16)
    Sgb = K.sb("Sgb", [128, 4, 128], BF16, nsub=4)
    Sh = K.sb("Sh", [128, 32, 128], F32, nsub=32)
    Shb = K.sb("Shb", [128, 128], BF16)
    KTb = [K.sb(f"KTb{l}", [128, 4, 256], BF16) for l in range(2)]
    Vb = [K.sb(f"Vb{l}", [128, 2, 512], BF16) for l in range(2)]
    stg = K.sb("stg", [128, 4, 256], F32)
    gbT = K.sb("gbT", [16, 2, N], F32)
    gtok = K.sb("gtok", [128, 4, 16], F32)
    btok = K.sb("btok", [128, 4, 16], F32)
    nbtok = K.sb("nbtok", [128, 4, 16], F32)
    esm = K.sb("esm", [128, 8], F32)
    c1 = K.sb("c1", [128, 16], F32)
    nalog = K.sb("nalog", [16, 1], F32)
    lb = K.sb("lb", [128, 32], F32)
    oml = K.sb("oml", [128, 32], F32)
    wab = [K.sb(f"wab{i}", [128, 128], BF16) for i in range(2)]
    wib = [K.sb(f"wib{i}", [128, 128], BF16) for i in range(2)]
    waf = [K.sb(f"waf{i}", [128, 2, 128], F32) for i in range(2)]
    def g4(name, dt=F32):
        return K.sb(name, [128, 4, 128], dt)
    Rt, DTs, DTc, egB, Pm, PTm, X2, XT2, TA = [g4(n) for n in
                                               ("Rt", "DTs", "DTc", "egB", "Pm", "PTm", "X2", "XT2", "TA")]
    atT, kdt, wTb, qdc, vnb, osq = [g4(n, BF16) for n in ("atT", "kdt", "wTb", "qdc", "vnb", "osq")]
    Rk = K.sb("Rk", [128, 4, 256], F32)
    UW = K.sb("UW", [128, 4, 256], F32)

    PB = [K.ps(f"PB{i}", [128, 512], F32) for i in range(4)]
    GP = [K.ps(f"GP{i}", [128, 512], F32) for i in range(3)]
    TB = K.ps("TB", [128, 1024], BF16)

    ident = cs[:, CI:CI + 128]
    onesf = cs[:, CO:CO + 128]
    Ucaus = cs[:, CUC:CUC + 128]
    Ustr = cs[:, CUS:CUS + 128]
    Lstr = cs[:, CLS:CLS + 128]

    def b4(ap2d):
        return ap2d.unsqueeze(1).broadcast_to([128, 4, 128])

    def pvc(name, i=0, w=1):
        return pv[:, PV[name] + i:PV[name] + i + w]

    K.dma("sp", pv[:], pvd, [], [pv], "pv")
    K.dma("sp", cs[:], cst, [], [cs], "cs")
    K.cp(ones_bf[:], onesf, [cs], [ones_bf])
    K.cp(ident_bf[:], ident, [cs], [ident_bf])
    K.cp(hm_bf[:], cs[:, CHM:CHM + 128], [cs], [hm_bf])
    K.act(c1[:], pvc("lru_lam", 0, 16), AF.Exp, [pv], [c1], scale=-1.0)
    K.act(c1[:], c1[:], AF.Ln, [c1], [c1], bias=1.0)
    K.ts(c1[:], c1[:], -8.0, None, ALU.mult, [c1], [c1])
    K.act(nalog[:], pv[0:16, PV["a_log"]:PV["a_log"] + 1], AF.Exp, [pv], [nalog])
    K.ts(nalog[:], nalog[:], -1.0, None, ALU.mult, [nalog], [nalog])
    K.tt(lb[:], pvc("lb_raw", 32, 32), pvc("lb_raw", 0, 32), ALU.subtract, [pv], [lb])
    K.act(lb[:], lb[:], AF.Sigmoid, [lb], [lb])
    K.ts(oml[:], lb[:], -1.0, 1.0, ALU.mult, [lb], [oml], op1=ALU.add)

    K.op("dve", "memset", [], [Sg], Sg[:, :, :], 0.0)
    K.op("dve", "memset", [], [Sh], Sh[:, :, :], 0.0)
    def rmsnorm(X, nk, n, wname, woff, dst):
        ps = GP[0]
        for kc in range(nk):
            s = sq[kc % 2]
            K.act(s[:, :n], X[:, kc, :n], AF.Square, [X.k(kc)], [s])
            K.mm(ps[:, :n], ones_bf[:], s[:, :n], [s, ones_bf], [ps], start=(kc == 0), stop=(kc == nk - 1))
        K.act(rstd[:, :n], ps[:, :n], AF.Sqrt, [ps], [rstd], scale=1.0 / (nk * 128), bias=EPS)
        K.op("dve", "reciprocal", [rstd], [rstd], out=rstd[:, :n], in_=rstd[:, :n])
        for kc in range(nk):
            K.stt(dst[:, kc, :n], X[:, kc, :n], pvc(wname, woff + kc), rstd[:, :n], ALU.mult, ALU.mult,
                  [X.k(kc), rstd, pv], [dst])

    scr = {}
    scr_off = [0]
    SCR_CH = 120 * 1024 * 1024
    wscrs = [nc.dram_tensor("wscr%d" % i, [SCR_CH], BF16, kind="Internal").ap() for i in range(4)]

    def proj(Wd, row0, nk, col0, ncols, rhs, n, consumer, rkeys=None, gcons=None):
        rk = [rhs] if rkeys is None else rkeys
        for g0 in range(0, ncols, 512):
            gcn = min(512, ncols - g0)
            nch = (gcn + 127) // 128
            nkt = (nk + 3) // 4
            for kt in range(nkt):
                kk = min(4, nk - kt * 4)
                slot = wsl[K.nsl % NSL]
                K.nsl += 1
                src = Wd[row0 + kt * 512:row0 + kt * 512 + kk * 128, col0 + g0:col0 + g0 + gcn].rearrange(
                    "(kc p) n -> p kc n", p=128)
                wkey = (str(Wd), row0 + kt * 512, col0 + g0, kk, gcn)
                if wkey not in scr:
                    sz_ = 128 * kk * gcn
                    if (scr_off[0] % SCR_CH) + sz_ > SCR_CH:
                        scr_off[0] = (scr_off[0] // SCR_CH + 1) * SCR_CH
                    off = scr_off[0]
                    scr_off[0] += sz_
                    ci_ = len(scr)
                    scr[wkey] = (off, ci_)
                    wscr = wscrs[off // SCR_CH]
                    o_ = off % SCR_CH
                    dstv = wscr[o_:o_ + sz_].rearrange("(p k n) -> p k n", p=128, k=kk)
                    K.dma("pool", dstv, src, [], [("scr", ci_), ("thr", ci_ % 6)], "scrc%d" % (ci_ % 6))
                off, ci_ = scr[wkey]
                wscr = wscrs[off // SCR_CH]
                o_ = off % SCR_CH
                srcv = wscr[o_:o_ + 128 * kk * gcn].rearrange("(p k n) -> p k n", p=128, k=kk)
                K.dma("sp", slot[:, :kk, :gcn], srcv, [("scr", ci_), ("thr", ci_ % 6)], [slot], slot.name)
                for c in range(nch):
                    m = min(128, gcn - c * 128)
                    for k in range(kk):
                        K.mm(PB[c][:m, :n], slot[:, k, c * 128:c * 128 + m], rhs[:, kt * 4 + k, :n], [slot] + rk,
                             [PB[c]], start=(kt == 0 and k == 0), stop=(kt == nkt - 1 and k == kk - 1))
            if gcons is not None:
                gcons([(g0 // 128 + c, PB[c], min(128, gcn - c * 128)) for c in range(nch)])
            else:
                for c in range(nch):
                    consumer(g0 // 128 + c, PB[c], min(128, gcn - c * 128))

    def add_resid(n):
        def f(c, ps, m):
            K.tt(xres[:, c, :n], xres[:, c, :n], ps[:, :n], ALU.add, [xres.k(c), ps], [xres.k(c)])
        return f

    def mem_project():
        K.dma("sp", xres[:, :, 0:256], memT.rearrange("(kc p) t -> p kc t", p=128), [], [xres], "xin")
        for l in range(2):
            rmsnorm(xres, 32, 256, "norm_mem_kv", 32 * l, hT)

            def ck(c, ps, m, l=l):
                K.cp(stg[:, c, :], ps[:, :256], [ps], [stg], eng="act")
                K.cp(KTb[l][:, c, :], ps[:, :256], [ps], [KTb[l]])
            proj(mem_w_k[l], 0, 32, 0, 512, hT, 256, ck)
            K.dma("sp", okT[l].rearrange("(c p) m -> p c m", p=128), stg[:], [stg], [], "okT")

            def cv(c, ps, m, l=l):
                K.cp(stg[:, c, :], ps[:, :256], [ps], [stg], eng="act")
            proj(mem_w_v[l], 0, 32, 0, 512, hT, 256, cv)
            for mc in range(2):
                for c in range(4):
                    K.tr(GP[mc][:, c * 128:(c + 1) * 128], stg[:, c, mc * 128:(mc + 1) * 128], ident, [stg, cs],
                         [GP[mc]])
                K.cp(Vb[l][:, mc, :], GP[mc][:, :], [GP[mc]], [Vb[l]])
                K.cp(f1[mc][:, :], GP[mc][:, :], [GP[mc]], [f1[mc]], eng="act")
                K.dma("sp", ov[l, mc * 128:(mc + 1) * 128, :], f1[mc][:, :], [f1[mc]], [], "ov")

    def mem_attend_prompt(l, n):
        rmsnorm(xres, 32, n, "norm_mem", 32 * l, hT)

        def cq(c, ps, m):
            K.ts(gA[:, c, :n], ps[:, :n], 128.0 ** -0.5, None, ALU.mult, [ps], [gA.k(c)])
        proj(mem_w_q[l], 0, 32, 0, 512, hT, n, cq)
        for h in range(4):
            for mc in range(2):
                K.mm(GP[mc][:, :n], KTb[l][:, h, mc * 128:(mc + 1) * 128], gA[:, h, :n], [KTb[l], gA.k(h)], [GP[mc]])
                K.act(gB[:, mc, :n], GP[mc][:, :n], AF.Exp, [GP[mc]], [gB.k(mc)])
            for mc in range(2):
                K.mm(GP[2][:, :n], ones_bf[:], gB[:, mc, :n], [ones_bf, gB.k(mc)], [GP[2]], start=(mc == 0),
                     stop=(mc == 1))
            for mc in range(2):
                K.mm(GP[0][:, :n], Vb[l][:, mc, h * 128:(h + 1) * 128], gB[:, mc, :n], [Vb[l], gB.k(mc)], [GP[0]],
                     start=(mc == 0), stop=(mc == 1))
            K.op("dve", "reciprocal", [GP[2]], [f1[0]], out=f1[0][:, :n], in_=GP[2][:, :n])
            K.tt(gC[:, h, :n], GP[0][:, :n], f1[0][:, :n], ALU.mult, [GP[0], f1[0]], [gC.k(h)])
        proj(mem_w_o[l], 0, 4, 0, 4096, gC, n, add_resid(n))

    def ffn(l, n):
        rmsnorm(xres, 32, n, "norm_ffn", 32 * l, hT)
        for g in range(8):
            def cu(c, ps, m):
                cc = c % 16
                t = f1[cc % 4]
                K.act(t[:, :n], ps[:, :n], AF.Relu, [ps], [t])
                K.tt(moT[:, cc, :n], t[:, :n], t[:, :n], ALU.mult, [t], [moT.k(cc)], eng="pool")
            proj(ffn_w_up[l], 0, 32, g * 2048, 2048, hT, n, cu)
            proj(ffn_w_down[l], g * 2048, 16, 0, 4096, moT, n, add_resid(n), rkeys=[moT.k(i) for i in range(16)])

    def conv_fm(ps, n, cst_tile, cidx, wname, wstride, widx, bias_ap, first_tile, par):
        x_ = xp[par]
        a_ = cacc[par]
        if first_tile:
            K.op("dve", "memset", [], [x_], x_[:, 0:3], 0.0)
        else:
            K.cp(x_[:, 0:3], cst_tile[:, cidx, :], [cst_tile], [x_])
        K.cp(x_[:, 3:3 + n], ps[:, :n], [ps], [x_], eng="act")
        K.cp(cst_tile[:, cidx, :], x_[:, n:n + 3], [x_], [cst_tile])
        K.ts(a_[:, :n], x_[:, 0:n], pvc(wname, widx), None, ALU.mult, [x_, pv], [a_])
        for tap in range(1, 4):
            K.stt(a_[:, :n], x_[:, tap:tap + n], pvc(wname, tap * wstride + widx), a_[:, :n], ALU.mult, ALU.add,
                  [x_, pv, a_], [a_])
        if bias_ap is not None:
            K.ts(a_[:, :n], a_[:, :n], bias_ap, None, ALU.add, [a_, pv], [a_])
        return a_

    def l2norm_to(src, dst, n, scale):
        s = sq[0]
        K.act(s[:, :n], src[:, :n], AF.Square, [src], [s])
        K.mm(GP[0][:, :n], ones_bf[:], s[:, :n], [s, ones_bf], [GP[0]])
        K.act(rstd[:, :n], GP[0][:, :n], AF.Sqrt, [GP[0]], [rstd], bias=EPS)
        K.op("dve", "reciprocal", [rstd], [rstd], out=rstd[:, :n], in_=rstd[:, :n])
        K.stt(dst, src[:, :n], scale, rstd[:, :n], ALU.mult, ALU.mult, [src, rstd], [])

    def mkbuf(tile, flatten=False):
        if flatten:
            fa = tile[:].rearrange("p a b -> p (a b)")
            return (tile, lambda lo, hi: fa[:, lo:hi])
        return (tile, lambda lo, hi: tile[:, lo:hi])

    def conv_group(items, n, cst_tile, cids, wname, wstride, bias_name, first, XP, CA):
        for c, (cg, ps, m) in enumerate(items):
            xt, xa = XP[c]
            if first:
                K.op("dve", "memset", [], [xt], xa(0, 3), 0.0)
            else:
                K.cp(xa(0, 3), cst_tile[:, cids[c], :], [cst_tile], [xt])
        for c, (cg, ps, m) in enumerate(items):
            xt, xa = XP[c]
            K.cp(xa(3, 3 + n), ps[:, :n], [ps], [xt], eng="act")
        for c in range(len(items)):
            xt, xa = XP[c]
            K.cp(cst_tile[:, cids[c], :], xa(n, n + 3), [xt], [cst_tile])
        for c in range(len(items)):
            xt, xa = XP[c]
            at, aa = CA[c]
            K.ts(aa(0, n), xa(0, n), pvc(wname, cids[c]), None, ALU.mult, [xt, pv], [at])
        for tap in range(1, 4):
            for c in range(len(items)):
                xt, xa = XP[c]
                at, aa = CA[c]
                K.stt(aa(0, n), xa(tap, tap + n), pvc(wname, tap * wstride + cids[c]), aa(0, n), ALU.mult, ALU.add,
                      [xt, pv, at], [at])
        if bias_name is not None:
            for c in range(len(items)):
                at, aa = CA[c]
                K.ts(aa(0, n), aa(0, n), pvc(bias_name, cids[c]), None, ALU.add, [at, pv], [at])

    def gqkv(items, which, hg, dst, scale, first, n):
        XP = [mkbuf(xp[0]), mkbuf(xp[1]), mkbuf(UW, True), mkbuf(Rk, True)]
        CA = [mkbuf(cacc[0]), mkbuf(cacc[1]), mkbuf(TA, True), mkbuf(X2, True)]
        SQ = [mkbuf(sq[0]), mkbuf(sq[1]), mkbuf(osq, True), mkbuf(vnb, True)]
        RS = [mkbuf(rstd), mkbuf(f1[0]), mkbuf(f1[1]), mkbuf(f1[2])]
        PSN = [(GP[0], 0), (GP[0], 256), (GP[1], 0), (GP[1], 256)]
        cids = [which * 16 + hg * 4 + c for c in range(4)]
        conv_group(items, n, convst, cids, "gdn_conv_w", 48, None, first, XP, CA)
        if scale is None:
            for c in range(4):
                at, aa = CA[c]
                K.act(dst[:, c, :n], aa(0, n), AF.Silu, [at], [dst.k(c)])
            return
        for c in range(4):
            at, aa = CA[c]
            K.act(aa(0, n), aa(0, n), AF.Silu, [at], [at])
        for c in range(4):
            at, aa = CA[c]
            st, sa = SQ[c]
            K.act(sa(0, n), aa(0, n), AF.Square, [at], [st])
        for c in range(4):
            st, sa = SQ[c]
            pt, po = PSN[c]
            K.mm(pt[:, po:po + n], ones_bf[:], sa(0, n), [st, ones_bf], [pt])
        for c in range(4):
            pt, po = PSN[c]
            rt, ra = RS[c]
            K.act(ra(0, n), pt[:, po:po + n], AF.Sqrt, [pt], [rt], bias=EPS)
        for c in range(4):
            rt, ra = RS[c]
            K.op("dve", "reciprocal", [rt], [rt], out=ra(0, n), in_=ra(0, n))
        for c in range(4):
            at, aa = CA[c]
            rt, ra = RS[c]
            K.stt(dst[:, c, :n], aa(0, n), scale, ra(0, n), ALU.mult, ALU.mult, [at, rt], [dst.k(c)])

    def gcf(items, hg, n):
        nch = n // 64
        TA_ = [Rt, DTc, Pm, X2]
        TB_ = [DTs, egB, PTm, XT2]
        FB = [mkbuf(t, True) for t in TA_]
        GB = [mkbuf(t, True) for t in TB_]
        R4 = range(4)
        for c, (cg, ps, m) in enumerate(items):
            K.act(FB[c][1](0, n), ps[:, :n], AF.Sigmoid, [ps], [FB[c][0]])
        for c in R4:
            h = hg * 4 + c
            t, f = FB[c]
            K.ts(f(0, n), f(0, n), oml[:, h:h + 1], lb[:, h:h + 1], ALU.mult, [t, oml, lb], [t], op1=ALU.add)
        for c in R4:
            t, f = FB[c]
            K.act(f(256, 256 + n), f(0, n), AF.Ln, [t], [t])
        for c in R4:
            t, f = FB[c]
            t2, g = GB[c]
            K.op("dve", "tensor_tensor_scan", [t, cs], [t2], out=g(0, n), data0=cs[:, CRM:CRM + n],
                 data1=f(256, 256 + n), initial=0.0, op0=ALU.mult, op1=ALU.add)
        for c in R4:
            t, f = FB[c]
            K.ts(f(0, n), f(0, n), -1.0, 1.0, ALU.mult, [t], [t], op1=ALU.add)
        for c in R4:
            t2, g = GB[c]
            K.act(g(256, 256 + n), g(0, n), AF.Exp, [t2], [t2])
        for c in R4:
            t2, g = GB[c]
            K.stt(gA[:, c, :n], f1[c][:, :n], 128.0 ** -0.5, g(256, 256 + n), ALU.mult, ALU.mult, [f1[c], t2],
                  [gA.k(c)])
        for c in R4:
            t, f = FB[c]
            t2, g = GB[c]
            K.act(f(256, 256 + n), g(0, n), AF.Exp, [t2], [t], scale=-1.0)
        for c in R4:
            t, f = FB[c]
            K.tt(f(256, 256 + n), f(256, 256 + n), f(0, n), ALU.mult, [t], [t])
        for c in R4:
            t, f = FB[c]
            K.cp(gE[:, c, :n], f(256, 256 + n), [t], [gE.k(c)], eng="act")
        for c in R4:
            t, f = FB[c]
            t2, g = GB[c]
            K.tt(gB[:, c, :n].rearrange("p (c j) -> p c j", j=64), f(256, 256 + n).rearrange("p (c j) -> p c j", j=64),
                 g(256 + 63, 256 + n)[:, ::64].unsqueeze(2).broadcast_to([128, nch, 64]), ALU.mult, [t, t2], [gB.k(c)])
        for c in R4:
            t2, g = GB[c]
            K.cp(f1[c][:, 0:nch], g(256 + 63, 256 + n)[:, ::64], [t2], [f1[c]])

    def gcx(items, lg, first, n):
        XP = [mkbuf(xp[0]), mkbuf(xp[1]), mkbuf(UW, True), mkbuf(Rk, True)]
        CA = [mkbuf(cacc[0]), mkbuf(cacc[1]), mkbuf(TA, True), mkbuf(X2, True)]
        SQ = [mkbuf(sq[0]), mkbuf(sq[1]), mkbuf(osq, True), mkbuf(vnb, True)]
        G_ = [mkbuf(t, True) for t in (Rt, DTs, DTc, egB)]
        H_ = [mkbuf(Pm, True), mkbuf(PTm, True), mkbuf(XT2, True), mkbuf(f1[3])]
        WF = [waf[0], waf[1], f1[0], f1[1]]
        PSG = [GP[0], GP[1], GP[2], GP[0]]
        R4 = range(4)
        nbs = [lg * 4 + c for c in R4]
        conv_group(items, n, lconvst, nbs, "lru_conv_w", 16, "lru_conv_b", first, XP, CA)
        for c in R4:
            wf = WF[c]
            wv = wf[:, :, :] if c < 2 else wf[:, 0:256].rearrange("p (t j) -> p t j", t=2)
            K.dma("sp", wv[:, 0, :], lru_w_a[nbs[c]], [], [wf], "wafq%d" % c)
            K.dma("sp", wv[:, 1, :], lru_w_i[nbs[c]], [], [wf], "wafq%d" % c)
            K.cp(atT[:, c, :], wv[:, 0, :], [wf], [atT], eng="pool")
            K.cp(kdt[:, c, :], wv[:, 1, :], [wf], [kdt], eng="pool")
        for c in R4:
            K.cp(SQ[c][1](0, n), CA[c][1](0, n), [CA[c][0]], [SQ[c][0]])
        for c in R4:
            K.mm(PSG[c][:, 0:n], atT[:, c, :], SQ[c][1](0, n), [atT, SQ[c][0]], [PSG[c]])
            K.mm(PSG[c][:, 256:256 + n], kdt[:, c, :], SQ[c][1](0, n), [kdt, SQ[c][0]], [PSG[c]])
            gt, g = G_[c]
            K.act(g(0, n), PSG[c][:, 0:n], AF.Sigmoid, [PSG[c], pv], [gt], bias=pvc("lru_b_a", nbs[c]))
            K.act(g(256, 256 + n), PSG[c][:, 256:256 + n], AF.Sigmoid, [PSG[c], pv], [gt], bias=pvc("lru_b_i", nbs[c]))
        for c in R4:
            gt, g = G_[c]
            ht, h = H_[c]
            K.act(h(0, n), g(0, n), AF.Exp, [gt, c1], [ht], scale=c1[:, nbs[c]:nbs[c] + 1])
        for c in R4:
            ht, h = H_[c]
            K.tt(h(256, 256 + n), h(0, n), h(0, n), ALU.mult, [ht], [ht])
        for c in R4:
            ht, h = H_[c]
            K.act(h(256, 256 + n), h(256, 256 + n), AF.Sqrt, [ht], [ht], scale=-1.0, bias=1.0)
            if first:
                K.op("dve", "memset", [], [ht], h(256, 257), 1.0)
        for c in R4:
            gt, g = G_[c]
            K.tt(g(256, 256 + n), g(256, 256 + n), CA[c][1](0, n), ALU.mult, [gt, CA[c][0]], [gt])
        for c in R4:
            gt, g = G_[c]
            ht, h = H_[c]
            K.tt(h(256, 256 + n), h(256, 256 + n), g(256, 256 + n), ALU.mult, [ht, gt], [ht])
        for c in R4:
            gt, g = G_[c]
            ht, h = H_[c]
            nb_ = nbs[c]
            if first:
                K.op("dve", "memset", [], [hst], hst[:, nb_:nb_ + 1], 0.0)
            K.op("dve", "tensor_tensor_scan", [ht, hst], [gt], out=g(0, n), data0=h(0, n), data1=h(256, 256 + n),
                 initial=hst[:, nb_:nb_ + 1], op0=ALU.mult, op1=ALU.add)
        for c in R4:
            gt, g = G_[c]
            nb_ = nbs[c]
            K.cp(hst[:, nb_:nb_ + 1], g(n - 1, n), [gt], [hst])
            K.tt(moT[:, 16 + nb_, :n], g(0, n), gD[:, c, :n], ALU.mult, [gt, gD.k(c)], [moT.k(16 + nb_)])

    def gdn_chunk(h0, ci):
        c0 = ci * 128
        g4_ = gtok[:, ci, h0:h0 + 4]
        b4_ = btok[:, ci, h0:h0 + 4]
        nb4_ = nbtok[:, ci, h0:h0 + 4]
        K.tt(Rt[:], g4_.unsqueeze(2).broadcast_to([128, 4, 128]), b4(Ucaus), ALU.mult, [gtok, cs], [Rt])
        Rf = Rt[:].rearrange("p h i -> p (h i)")
        K.mm(GP[0][:, :], Lstr, Rf, [cs, Rt], [GP[0]])
        K.mm(GP[1][:, :], onesf, Rf, [cs, Rt], [GP[1]])
        K.mm(GP[2][:, 0:4], Ucaus, g4_, [cs, gtok], [GP[2]])
        K.mm(GP[2][:, 4:8], Lstr, g4_, [cs, gtok], [GP[2]])
        K.act(DTc[:].rearrange("p h i -> p (h i)"), GP[0][:, :], AF.Exp, [GP[0]], [DTc])
        K.act(egB[:].rearrange("p h i -> p (h i)"), GP[1][:, :], AF.Exp, [GP[1]], [egB])
        K.act(esm[:, 0:8], GP[2][:, 0:8], AF.Exp, [GP[2]], [esm])
        K.tt(DTs[:], DTc[:], b4(Ustr), ALU.mult, [DTc, cs], [DTs])
        K.tt(DTc[:], DTc[:], b4(Ucaus), ALU.mult, [DTc, cs], [DTc])
        for hh in range(4):
            kc_ = gB[:, hh, c0:c0 + 128]
            qc_ = gA[:, hh, c0:c0 + 128]
            K.mm(GP[0][:, hh * 128:(hh + 1) * 128], kc_, kc_, [gB.k(hh)], [GP[0]])
            K.mm(GP[1][:, hh * 128:(hh + 1) * 128], kc_, qc_, [gB.k(hh), gA.k(hh)], [GP[1]])
            K.tr(TB[:, hh * 128:(hh + 1) * 128], gC[:, hh, c0:c0 + 128], ident_bf[:], [gC.k(hh), ident_bf], [TB])
            K.tr(TB[:, 512 + hh * 128:512 + (hh + 1) * 128], kc_, ident_bf[:], [gB.k(hh), ident_bf], [TB])
        G0 = GP[0][:, :].rearrange("p (h i) -> p h i", h=4)
        G1 = GP[1][:, :].rearrange("p (h i) -> p h i", h=4)
        G2 = GP[2][:, :].rearrange("p (h i) -> p h i", h=4)
        K.tt(Pm[:], G0, DTs[:], ALU.mult, [GP[0], DTs], [Pm])
        K.tt(Pm[:], Pm[:], nb4_.unsqueeze(2).broadcast_to([128, 4, 128]), ALU.mult, [Pm, nbtok], [Pm])
        K.tt(atT[:], G1, DTc[:], ALU.mult, [GP[1], DTc], [atT])
        TBv = TB[:, 0:512].rearrange("p (h i) -> p h i", h=4)
        TBk = TB[:, 512:1024].rearrange("p (h i) -> p h i", h=4)
        K.cp(Rk[:, :, 0:128], TBv, [TB], [Rk], eng="act")
        K.tt(Rk[:, :, 128:256], TBk, esm[:, 0:4].unsqueeze(2).broadcast_to([128, 4, 128]), ALU.mult, [TB, esm], [Rk])
        K.tt(kdt[:], TBk, esm[:, 4:8].unsqueeze(2).broadcast_to([128, 4, 128]), ALU.mult, [TB, esm], [kdt])
        K.tt(qdc[:], gA[:, :, c0:c0 + 128], egB[:], ALU.mult, [gA, egB], [qdc])
        for hh in range(4):
            K.tr(GP[2][:, hh * 128:(hh + 1) * 128], Pm[:, hh, :], ident, [Pm, cs], [GP[2]])
        K.cp(PTm[:], G2, [GP[2]], [PTm], eng="act")
        K.tt(TA[:], Pm[:], b4(ident), ALU.add, [Pm, cs], [TA])
        X, XT = Pm, PTm
        Xn, XTn = X2, XT2
        for st in range(6):
            last = (st == 5)
            for hh in range(4):
                K.mm(GP[1][:, hh * 128:(hh + 1) * 128], X[:, hh, :], XT[:, hh, :], [X, XT], [GP[1]])
            if not last:
                for hh in range(4):
                    K.mm(GP[0][:, hh * 128:(hh + 1) * 128], XT[:, hh, :], X[:, hh, :], [X, XT], [GP[0]])
            K.cp(XTn[:], G1, [GP[1]], [XTn], eng="act")
            if not last:
                K.cp(Xn[:], G0, [GP[0]], [Xn])
            for hh in range(4):
                K.mm(GP[2][:, hh * 128:(hh + 1) * 128], XTn[:, hh, :], TA[:, hh, :], [XTn, TA], [GP[2]])
            K.tt(TA[:], TA[:], G2, ALU.add, [TA, GP[2]], [TA])
            X, XT, Xn, XTn = Xn, XTn, X, XT
        for hh in range(4):
            bank = GP[hh // 2]
            K.mm(bank[:, (hh % 2) * 256:(hh % 2) * 256 + 256], TA[:, hh, :], Rk[:, hh, :], [TA, Rk], [bank])
        for b2 in range(2):
            K.tt(UW[:, 2 * b2:2 * b2 + 2, :], GP[b2][:, :].rearrange("p (h c) -> p h c", h=2),
                 b4_[:, 2 * b2:2 * b2 + 2].unsqueeze(2).broadcast_to([128, 2, 256]), ALU.mult, [GP[b2], btok], [UW])
        for hh in range(4):
            K.tr(GP[2][:, hh * 128:(hh + 1) * 128], UW[:, hh, 128:256], ident, [UW, cs], [GP[2]])
        K.cp(wTb[:], G2, [GP[2]], [wTb], eng="act")
        for hh in range(4):
            h = h0 + hh
            K.mm(GP[0][:, hh * 128:(hh + 1) * 128], wTb[:, hh, :], Sgb[:, hh, :], [wTb, Sgb.k(hh)], [GP[0]])
        K.tt(vnb[:], UW[:, :, 0:128], G0, ALU.subtract, [UW, GP[0]], [vnb])
        for hh in range(4):
            h = h0 + hh
            K.mm(GP[1][:, hh * 128:(hh + 1) * 128], Sgb[:, hh, :], qdc[:, hh, :], [Sgb.k(hh), qdc], [GP[1]], start=True,
                 stop=False)
            K.mm(GP[1][:, hh * 128:(hh + 1) * 128], vnb[:, hh, :], atT[:, hh, :], [vnb, atT], [GP[1]], start=False,
                 stop=True)
            K.mm(GP[2][:, hh * 128:(hh + 1) * 128], kdt[:, hh, :], vnb[:, hh, :], [kdt, vnb], [GP[2]])
        for hh in range(4):
            h = h0 + hh
            K.stt(Sg[:, h, :], Sg[:, h, :], egB[:, hh, 127:128], GP[2][:, hh * 128:(hh + 1) * 128], ALU.mult, ALU.add,
                  [Sg.k(h), egB, GP[2]], [Sg.k(h)])
        K.cp(Sgb[:, :, :], Sg[:, h0:h0 + 4, :], [Sg.k(h0 + i) for i in range(4)], [Sgb], eng="act")
        K.act(osq[:].rearrange("p h i -> p (h i)"), GP[1][:, :], AF.Square, [GP[1]], [osq])
        K.mm(GP[0][:, :], ones_bf[:], osq[:].rearrange("p h i -> p (h i)"), [ones_bf, osq], [GP[0]])
        K.act(f1[0][:, :], GP[0][:, :], AF.Sqrt, [GP[0]], [f1[0]], scale=1.0 / 128, bias=EPS)
        K.op("dve", "reciprocal", [f1[0]], [f1[0]], out=f1[0][:, :], in_=f1[0][:, :])
        K.tt(f1[1][:, :], GP[1][:, :], f1[0][:, :], ALU.mult, [GP[1], f1[0]], [f1[1]])
        K.stt(moT[:, h0:h0 + 4, c0:c0 + 128], f1[1][:, :].rearrange("p (h i) -> p h i", h=4), pvc("gdn_norm_w"),
              gD[:, :, c0:c0 + 128], ALU.mult, ALU.mult, [f1[1], pv, gD], [moT.k(h0 + i) for i in range(4)])

    def gb_proj(n):
        def cb(c, ps, m):
            K.act(gbT[:, 1, :n], ps[0:16, :n], AF.Sigmoid, [ps], [gbT])
        proj(ab_w_in, 0, 32, 8192, 16, hT, n, cb)

        def ca(c, ps, m):
            K.act(gbT[:, 0, :n], ps[0:16, :n], AF.Exp, [ps], [gbT], bias=pv[0:16, PV["dt_bias"]:PV["dt_bias"] + 1])
            K.act(gbT[:, 0, :n], gbT[:, 0, :n], AF.Ln, [gbT], [gbT], bias=1.0)
            K.ts(gbT[:, 0, :n], gbT[:, 0, :n], nalog[:, 0:1], None, ALU.mult, [gbT, nalog], [gbT])
        proj(ab_w_in, 0, 32, 8208, 16, hT, n, ca)

    def ab_mixer_prompt(ti, n):
        first = (ti == 0)
        rmsnorm(xres, 32, n, "norm_mix", 0, hT)

        gb_proj(n)
        for ci in range(n // 128):
            K.tr(GP[0][:, 0:16], gbT[:, 0, ci * 128:(ci + 1) * 128], cs[0:16, CI:CI + 16], [gbT, cs], [GP[0]])
            K.tr(GP[0][:, 16:32], gbT[:, 1, ci * 128:(ci + 1) * 128], cs[0:16, CI:CI + 16], [gbT, cs], [GP[0]])
            K.cp(gtok[:, ci, :], GP[0][:, 0:16], [GP[0]], [gtok])
            K.cp(btok[:, ci, :], GP[0][:, 16:32], [GP[0]], [btok])
            K.ts(nbtok[:, ci, :], GP[0][:, 16:32], -1.0, None, ALU.mult, [GP[0]], [nbtok])
        par = [0]
        for hg in range(4):
            for which, dst, scale in ((0, gA, 128.0 ** -0.5), (1, gB, 1.0), (2, gC, None)):
                def cqkv(c, ps, m, which=which, dst=dst, scale=scale):
                    cidx = which * 16 + hg * 4 + c
                    a_ = conv_fm(ps, n, convst, cidx, "gdn_conv_w", 48, cidx, None, first, par[0])
                    par[0] ^= 1
                    if scale is None:
                        K.act(dst[:, c, :n], a_[:, :n], AF.Silu, [a_], [dst.k(c)])
                    else:
                        K.act(a_[:, :n], a_[:, :n], AF.Silu, [a_], [a_])
                        l2norm_to(a_, dst[:, c, :n], n, scale)
                        K.ops[-1]["W"] = _keys([dst.k(c)])
                proj(ab_w_in, 0, 32, which * 2048 + hg * 512, 512, hT, n, None,
                     gcons=lambda items, which=which, dst=dst, scale=scale: gqkv(items, which, hg, dst, scale, first, n))

            def cz(c, ps, m):
                K.act(gD[:, c, :n], ps[:, :n], AF.Silu, [ps], [gD.k(c)])
            proj(ab_w_in, 0, 32, 6144 + hg * 512, 512, hT, n, cz)
            K.cp(Sgb[:, :, :], Sg[:, hg * 4:hg * 4 + 4, :], [Sg.k(hg * 4 + i) for i in range(4)], [Sgb], eng="act")
            for ci in range(n // 128):
                gdn_chunk(hg * 4, ci)
        for lg in range(4):
            def cy(c, ps, m):
                K.act(gD[:, c, :n], ps[:, :n], AF.Gelu_apprx_tanh, [ps], [gD.k(c)])
            proj(ab_w_in, 0, 32, 10272 + lg * 512, 512, hT, n, cy)

            def cx(c, ps, m):
                nb_ = lg * 4 + c
                p2 = nb_ % 2
                a_ = conv_fm(ps, n, lconvst, nb_, "lru_conv_w", 16, nb_, pvc("lru_conv_b", nb_), first, par[0])
                par[0] ^= 1
                K.dma("sp", waf[p2][:, 0, :], lru_w_a[nb_], [], [waf[p2]], "waf%d" % p2)
                K.dma("sp", waf[p2][:, 1, :], lru_w_i[nb_], [], [waf[p2]], "waf%d" % p2)
                K.cp(wab[p2][:], waf[p2][:, 0, :], [waf[p2]], [wab[p2]], eng="pool")
                K.cp(wib[p2][:], waf[p2][:, 1, :], [waf[p2]], [wib[p2]], eng="pool")
                xb = sq[1]
                K.cp(xb[:, :n], a_[:, :n], [a_], [xb])
                K.mm(GP[0][:, :n], wab[p2][:], xb[:, :n], [wab[p2], xb], [GP[0]])
                K.mm(GP[1][:, :n], wib[p2][:], xb[:, :n], [wib[p2], xb], [GP[1]])
                ga, gi, aa, bb = f1[0], f1[1], f1[2], f1[3]
                K.act(ga[:, :n], GP[0][:, :n], AF.Sigmoid, [GP[0], pv], [ga], bias=pvc("lru_b_a", nb_))
                K.act(gi[:, :n], GP[1][:, :n], AF.Sigmoid, [GP[1], pv], [gi], bias=pvc("lru_b_i", nb_))
                K.act(aa[:, :n], ga[:, :n], AF.Exp, [ga, c1], [aa], scale=c1[:, nb_:nb_ + 1])
                K.tt(bb[:, :n], aa[:, :n], aa[:, :n], ALU.mult, [aa], [bb])
                K.act(bb[:, :n], bb[:, :n], AF.Sqrt, [bb], [bb], scale=-1.0, bias=1.0)
                if first:
                    K.op("dve", "memset", [], [bb], bb[:, 0:1], 1.0)
                K.tt(gi[:, :n], gi[:, :n], a_[:, :n], ALU.mult, [gi, a_], [gi])
                K.tt(bb[:, :n], bb[:, :n], gi[:, :n], ALU.mult, [bb, gi], [bb])
                if first:
                    K.op("dve", "memset", [], [hst], hst[:, nb_:nb_ + 1], 0.0)
                K.op("dve", "tensor_tensor_scan", [aa, bb, hst], [ga], out=ga[:, :n], data0=aa[:, :n],
                     data1=bb[:, :n], initial=hst[:, nb_:nb_ + 1], op0=ALU.mult, op1=ALU.add)
                K.cp(hst[:, nb_:nb_ + 1], ga[:, n - 1:n], [ga], [hst])
                K.tt(moT[:, 16 + nb_, :n], ga[:, :n], gD[:, c, :n], ALU.mult, [ga, gD.k(c)], [moT.k(16 + nb_)])
            proj(ab_w_in, 0, 32, 8224 + lg * 512, 512, hT, n, None, gcons=lambda items: gcx(items, lg, first, n))
        proj(ab_w_out, 0, 32, 0, 4096, moT, n, add_resid(n))

    def c_mixer_prompt(ti, n):
        first = (ti == 0)
        rmsnorm(xres, 32, n, "norm_mix", 32, hT)
        nch = n // 64
        for hg in range(8):
            def cq(c, ps, m):
                K.act(f1[c][:, :n], ps[:, :n], AF.Silu, [ps], [f1[c]])
            proj(c_w_in, 0, 32, hg * 512, 512, hT, n, cq)

            def cf(c, ps, m):
                h = hg * 4 + c
                f_, lg_, gc_ = cacc[0], cacc[1], xp[0]
                K.act(f_[:, :n], ps[:, :n], AF.Sigmoid, [ps], [f_])
                K.ts(f_[:, :n], f_[:, :n], oml[:, h:h + 1], lb[:, h:h + 1], ALU.mult, [f_, oml, lb], [f_], op1=ALU.add)
                K.act(lg_[:, :n], f_[:, :n], AF.Ln, [f_], [lg_])
                K.op("dve", "tensor_tensor_scan", [lg_, cs], [gc_], out=gc_[:, :n], data0=cs[:, CRM:CRM + n],
                     data1=lg_[:, :n], initial=0.0, op0=ALU.mult, op1=ALU.add)
                K.ts(f_[:, :n], f_[:, :n], -1.0, 1.0, ALU.mult, [f_], [f_], op1=ALU.add)
                eg_ = xp[1]
                K.act(eg_[:, :n], gc_[:, :n], AF.Exp, [gc_], [eg_])
                K.stt(gA[:, c, :n], f1[c][:, :n], 128.0 ** -0.5, eg_[:, :n], ALU.mult, ALU.mult, [f1[c], eg_],
                      [gA.k(c)])
                K.act(lg_[:, :n], gc_[:, :n], AF.Exp, [gc_], [lg_], scale=-1.0)
                K.tt(lg_[:, :n], lg_[:, :n], f_[:, :n], ALU.mult, [lg_, f_], [lg_])
                K.cp(gE[:, c, :n], lg_[:, :n], [lg_], [gE.k(c)], eng="act")
                K.tt(gB[:, c, :n].rearrange("p (c j) -> p c j", j=64), lg_[:, :n].rearrange("p (c j) -> p c j", j=64),
                     eg_[:, 63:n:64].unsqueeze(2).broadcast_to([128, nch, 64]), ALU.mult, [lg_, eg_], [gB.k(c)])
                K.cp(f1[c][:, 0:nch], eg_[:, 63:n:64], [eg_], [f1[c]])
            proj(c_w_in, 0, 32, 4096 + hg * 512, 512, hT, n, None, gcons=lambda items: gcf(items, hg, n))

            def ci_(c, ps, m):
                K.cp(gC[:, c, :n], ps[:, :n], [ps], [gC.k(c)], eng="act")
            proj(c_w_in, 0, 32, 8192 + hg * 512, 512, hT, n, ci_)

            def cgz(c, ps, m):
                K.act(gD[:, c, :n], ps[:, :n], AF.Silu, [ps], [gD.k(c)])
            proj(c_w_in, 0, 32, 12288 + hg * 512, 512, hT, n, cgz)
            for c in range(4):
                h = hg * 4 + c
                K.cp(Shb[:, :], Sh[:, h, :], [Sh.k(h)], [Shb], eng="act")
                for bi in range(n // 128):
                    K.tr(TB[:, bi * 128:(bi + 1) * 128], gC[:, c, bi * 128:(bi + 1) * 128], ident_bf[:],
                         [gC.k(c), ident_bf], [TB])
                    K.tr(TB[:, 512 + bi * 128:512 + (bi + 1) * 128], gB[:, c, bi * 128:(bi + 1) * 128], ident_bf[:],
                         [gB.k(c), ident_bf], [TB])
                nb_ = n // 128
                K.cp(atT[:, 0:nb_, :], TB[:, 0:n].rearrange("p (b d) -> p b d", b=nb_), [TB], [atT])
                K.cp(kdt[:, 0:nb_, :], TB[:, 512:512 + n].rearrange("p (b d) -> p b d", b=nb_), [TB], [kdt], eng="act")
                for bi in range(n // 128):
                    sl_ = slice(bi * 128, (bi + 1) * 128)
                    K.mm(GP[0][:, sl_], gE[:, c, sl_], gA[:, c, sl_], [gE.k(c), gA.k(c)], [GP[0]])
                K.tt(qdc[:, 0:nb_, :], GP[0][:, 0:n].rearrange("p (b i) -> p b i", b=nb_),
                     hm_bf[:].unsqueeze(1).broadcast_to([128, nb_, 128]), ALU.mult, [GP[0], hm_bf], [qdc])
                for cj in range(nch):
                    bi, half = cj // 2, cj % 2
                    cols = slice(cj * 64, (cj + 1) * 64)
                    K.mm(GP[1][:, cols], Shb[:, :], gA[:, c, cols], [Shb, gA.k(c)], [GP[1]], start=True,
                         stop=False)
                    K.mm(GP[1][:, cols], atT[:, bi, :], qdc[:, bi, half * 64:(half + 1) * 64], [atT, qdc], [GP[1]],
                         start=False, stop=True)
                    pr = slice(half * 64, (half + 1) * 64)
                    K.mm(GP[2][:, 0:128], kdt[pr, bi, :], atT[pr, bi, :], [kdt, atT], [GP[2]])
                    K.stt(Sh[:, h, :], Sh[:, h, :], f1[c][:, cj:cj + 1], GP[2][:, 0:128], ALU.mult, ALU.add,
                          [Sh.k(h), f1[c], GP[2]], [Sh.k(h)])
                    K.cp(Shb[:, :], Sh[:, h, :], [Sh.k(h)], [Shb], eng="act")
                s_ = sq[0]
                K.act(s_[:, :n], GP[1][:, :n], AF.Square, [GP[1]], [s_])
                K.mm(GP[0][:, :n], ones_bf[:], s_[:, :n], [ones_bf, s_], [GP[0]])
                K.act(rstd[:, :n], GP[0][:, :n], AF.Sqrt, [GP[0]], [rstd], scale=1.0 / 128, bias=EPS)
                K.op("dve", "reciprocal", [rstd], [rstd], out=rstd[:, :n], in_=rstd[:, :n])
                K.tt(cacc[0][:, :n], GP[1][:, :n], rstd[:, :n], ALU.mult, [GP[1], rstd], [cacc[0]])
                K.stt(moT[:, h, :n], cacc[0][:, :n], pvc("hgrn_norm_w"), gD[:, c, :n], ALU.mult, ALU.mult,
                      [cacc[0], pv, gD.k(c)], [moT.k(h)])
        proj(c_w_out, 0, 32, 0, 4096, moT, n, add_resid(n))

    mem_project()
    for ti in range(NT):
        n = N
        K.dma("sp", xres[:, :, :], xT[:, ti * N:(ti + 1) * N].rearrange("(kc p) t -> p kc t", p=128), [], [xres],
              "xin")
        import os
        _st = int(os.environ.get("KSTOP", "9"))
        if _st >= 1:
            ab_mixer_prompt(ti, n)
        if _st >= 2:
            mem_attend_prompt(0, n)
        if _st >= 3:
            ffn(0, n)
        if _st >= 4:
            c_mixer_prompt(ti, n)
        if _st >= 5:
            mem_attend_prompt(1, n)
        if _st >= 6:
            ffn(1, n)
        ps = GP[0]
        for kc in range(32):
            s = sq[kc % 2]
            K.act(s[:, :n], xres[:, kc, :n], AF.Square, [xres.k(kc)], [s])
            K.mm(ps[:, :n], ones_bf[:], s[:, :n], [s, ones_bf], [ps], start=(kc == 0), stop=(kc == 31))
        K.act(rstd[:, :n], ps[:, :n], AF.Sqrt, [ps], [rstd], scale=1.0 / 4096, bias=EPS)
        K.op("dve", "reciprocal", [rstd], [rstd], out=rstd[:, :n], in_=rstd[:, :n])
        for kc in range(32):
            K.stt(xres[:, kc, :n], xres[:, kc, :n], pvc("norm_final", kc), rstd[:, :n], ALU.mult, ALU.mult,
                  [xres.k(kc), rstd, pv], [xres.k(kc)])
        K.dma("sp", yT[:, ti * N:(ti + 1) * N].rearrange("(kc p) t -> p kc t", p=128), xres[:, :, :], [xres], [],
              "yout")
    K.dma("sp", ogc, convst[:].rearrange("p c t -> p (c t)"), [convst], [], "so")
    K.dma("sp", olc, lconvst[:].rearrange("p c t -> p (c t)"), [lconvst], [], "so")
    K.dma("sp", ol, hst[:], [hst], [], "so")
    K.dma("sp", og.rearrange("h k v -> k h v"), Sg[:], [Sg], [], "so")
    K.dma("sp", oh.rearrange("h k v -> k h v"), Sh[:], [Sh], [], "so")
    AX = mybir.AxisListType.X
    XR = xres
    xs = K.sb("xs", [128, 32, TS], F32, nsub=32)
    hs = K.sb("hs", [128, 32, TS], BF16)
    ms = K.sb("ms", [128, 32, TS], BF16, nsub=32)
    xn = K.sb("xn", [128, 48, TS], F32)
    lxn = K.sb("lxn", [128, 16, TS], F32)
    xres, hT, moT = xs, hs, ms
    n = TS
    XRall = [XR]
    id16 = cs[0:16, CI:CI + 16]
    ones16 = cs[0:16, CO:CO + 128]

    def flat(t):
        return t[:].rearrange("p a b -> p (a b)")

    QKV = flat(UW)
    aB = f1[1][:, 0:256].rearrange("p (h b) -> p h b", h=16)
    bB = f1[1][:, 256:512].rearrange("p (h b) -> p h b", h=16)
    OG = f1[2][:, 0:256]
    YG = f1[3][:, 0:256].rearrange("p (c b) -> p c b", c=16)
    H0 = flat(DTs)[:, 0:256].rearrange("p (c b) -> p c b", c=16)
    HN = flat(DTc)[:, 0:256].rearrange("p (c b) -> p c b", c=16)

    def pcv(c):
        return XR[:, c // 2, 128 + (c % 2) * 64:128 + (c % 2) * 64 + 48].rearrange("p (t b) -> p t b", t=3)

    def lpcv(c):
        return XR[:, 24 + c // 2, 128 + (c % 2) * 64:128 + (c % 2) * 64 + 48].rearrange("p (t b) -> p t b", t=3)

    def bcast_rows(dst_ps, src16, b):
        pass

    K.dma("sp", xs[:, :, :], xsT.rearrange("(kc p) t -> p kc t", p=128), [], [xs], "xin")
    for half in range(2):
        K.dma("sp", XR[:, 0:24, 128 + half * 64:128 + half * 64 + 48],
              sgc.rearrange("(r two p) t b -> p r two (t b)", two=2, p=128)[:, :, half, :], [], XRall, "stin")
        K.dma("sp", XR[:, 24:32, 128 + half * 64:128 + half * 64 + 48],
              slc.rearrange("(r two p) t b -> p r two (t b)", two=2, p=128)[:, :, half, :], [], XRall, "stin")
    K.dma("sp", H0, sl.rearrange("(c p) b -> p c b", p=128), [], [DTs], "stin2")
    K.dma("sp", osgc[:, 0:2, :], sgc[:, 1:3, :], [], [], "cpy")
    K.dma("sp", oslc[:, 0:2, :], slc[:, 1:3, :], [], [], "cpy")

    def conv_s(ps, pv_, xn_t, cidx, wname, wstride, bias_ap):
        a_ = cacc[0]
        K.cp(xn_t[:, cidx, :], ps[:, :n], [ps], [xn_t], eng="act")
        K.ts(a_[:, :n], pv_[:, 0, :], pvc(wname, cidx), None, ALU.mult, XRall + [pv], [a_])
        for tap in (1, 2):
            K.stt(a_[:, :n], pv_[:, tap, :], pvc(wname, tap * wstride + cidx), a_[:, :n], ALU.mult, ALU.add,
                  XRall + [pv, a_], [a_])
        K.stt(a_[:, :n], xn_t[:, cidx, :], pvc(wname, 3 * wstride + cidx), a_[:, :n], ALU.mult, ALU.add,
              [xn_t, pv, a_], [a_])
        if bias_ap is not None:
            K.ts(a_[:, :n], a_[:, :n], bias_ap, None, ALU.add, [a_, pv], [a_])
        return a_

    def row_bcast(dst_ps, src_tile, msk_tile, b, extraR):
        K.ts(msk_tile[0:16, :, :], src_tile[0:16, :, :], cs[0:16, CI + b:CI + b + 1], None, ALU.mult,
             [src_tile, cs] + extraR, [msk_tile])
        K.mm(dst_ps[:, :], ones16, msk_tile[0:16, :, :].rearrange("p a b -> p (a b)"), [cs, msk_tile], [dst_ps])

    def v4(ap2d):
        return ap2d.rearrange("p (h i) -> p h i", h=4)

    def ab_mixer_sample():
        rmsnorm(xres, 32, n, "norm_mix", 0, hT)
        gb_proj(n)
        bc = f1[0]
        K.tt(bc[0:16, :].rearrange("p (w h b) -> p w h b", w=2, h=16),
             id16.unsqueeze(1).unsqueeze(3).broadcast_to([16, 2, 16, 16]),
             gbT[:, :, 0:16].unsqueeze(2).broadcast_to([16, 2, 16, 16]), ALU.mult, [cs, gbT], [bc])
        K.mm(GP[0][:, :], ones16, bc[0:16, :], [cs, bc], [GP[0]])
        K.act(f1[1][:, 0:256], GP[0][:, 0:256], AF.Exp, [GP[0]], [f1[1]])
        K.cp(f1[1][:, 256:512], GP[0][:, 256:512], [GP[0]], [f1[1]])
        for hg in range(4):
            for which, scale in ((0, 128.0 ** -0.5), (1, 1.0), (2, None)):
                def cqkv(c, ps, m, which=which, scale=scale):
                    cidx = which * 16 + hg * 4 + c
                    h = hg * 4 + c
                    a_ = conv_s(ps, pcv(cidx), xn, cidx, "gdn_conv_w", 48, None)
                    dst = QKV[:, which * 256 + h * 16:which * 256 + h * 16 + 16]
                    if scale is None:
                        K.act(dst, a_[:, :n], AF.Silu, [a_], [UW])
                    else:
                        K.act(a_[:, :n], a_[:, :n], AF.Silu, [a_], [a_])
                        l2norm_to(a_, dst, n, scale)
                        K.ops[-1]["W"] = _keys([UW])
                proj(ab_w_in, 0, 32, which * 2048 + hg * 512, 512, hT, n, cqkv)

            def cz(c, ps, m):
                h = hg * 4 + c
                K.act(QKV[:, 768 + h * 16:768 + h * 16 + 16], ps[:, :n], AF.Silu, [ps], [UW])
            proj(ab_w_in, 0, 32, 6144 + hg * 512, 512, hT, n, cz)
        K.dma("sp", osgc.rearrange("(c p) t b -> p c t b", p=128)[:, :, 2, :], xn[:], [xn], [], "so2")
        for hg in range(4):
            h0 = hg * 4
            for hh in range(4):
                h = h0 + hh
                K.tr(GP[2][0:16, hh * 128:(hh + 1) * 128], QKV[:, 256 + h * 16:256 + h * 16 + 16], ident, [UW, cs],
                     [GP[2]])
                K.tr(GP[0][0:16, hh * 128:(hh + 1) * 128], QKV[:, h * 16:h * 16 + 16], ident, [UW, cs], [GP[0]])
            K.cp(Pm[0:16, :, :], v4(GP[2][0:16, :]), [GP[2]], [Pm])
            K.cp(PTm[0:16, :, :], v4(GP[0][0:16, :]), [GP[0]], [PTm])
            for b in range(TS):
                bufk = [XR.k(i) for i in (range(0, 4) if b % 2 == 0 else range(4, 8))]
                r0 = 0 if b % 2 == 0 else 4
                ST4 = XR[:, r0:r0 + 4, 0:128]
                K.dma("sp", ST4, sg[b, h0:h0 + 4].rearrange("h v k -> v h k"), [], bufk, "stg%d" % (b % 2))
                row_bcast(PB[0], Pm, X2, b, [])
                row_bcast(PB[1], PTm, XT2, b, [])
                kBv, qBv = v4(PB[0][:, :]), v4(PB[1][:, :])
                a4 = aB[:, h0:h0 + 4, b]
                be4 = bB[:, h0:h0 + 4, b]
                vcol = QKV[:, 512:768].rearrange("p (h b) -> p h b", h=16)[:, h0:h0 + 4, b]
                K.tt(TA[:], ST4, kBv, ALU.mult, bufk + [PB[0]], [TA])
                K.op("dve", "reduce_sum", [TA], [esm], out=esm[:, 0:4], in_=TA[:], axis=AX)
                K.tt(esm[:, 0:4], esm[:, 0:4], a4, ALU.mult, [esm, f1[1]], [esm])
                K.tt(esm[:, 4:8], vcol, esm[:, 0:4], ALU.subtract, [UW, esm], [esm])
                K.tt(esm[:, 4:8], esm[:, 4:8], be4, ALU.mult, [esm, f1[1]], [esm])
                K.tt(TA[:], kBv, esm[:, 4:8].unsqueeze(2).broadcast_to([128, 4, 128]), ALU.mult, [PB[0], esm], [TA])
                K.tt(ST4, ST4, a4.unsqueeze(2).broadcast_to([128, 4, 128]), ALU.mult, bufk + [f1[1]], bufk)
                K.tt(ST4, ST4, TA[:], ALU.add, bufk + [TA], bufk)
                K.tt(TA[:], ST4, qBv, ALU.mult, bufk + [PB[1]], [TA])
                K.op("dve", "reduce_sum", [TA], [f1[2]], out=OG.rearrange("p (h b) -> p h b", h=16)[:, h0:h0 + 4, b],
                     in_=TA[:], axis=AX)
                K.dma("sp", osg[b, h0:h0 + 4].rearrange("h v k -> v h k"), ST4, bufk, [], "sto%d" % (b % 2))
        K.act(sq[0][:, 0:256], OG, AF.Square, [f1[2]], [sq[0]])
        K.mm(GP[0][:, 0:256], ones_bf[:], sq[0][:, 0:256], [ones_bf, sq[0]], [GP[0]])
        K.act(rstd[:, 0:256], GP[0][:, 0:256], AF.Sqrt, [GP[0]], [rstd], scale=1.0 / 128, bias=EPS)
        K.op("dve", "reciprocal", [rstd], [rstd], out=rstd[:, 0:256], in_=rstd[:, 0:256])
        K.tt(cacc[1][:, 0:256], OG, rstd[:, 0:256], ALU.mult, [f1[2], rstd], [cacc[1]])
        K.stt(moT[:, 0:16, :], cacc[1][:, 0:256].rearrange("p (h b) -> p h b", h=16), pvc("gdn_norm_w"),
              QKV[:, 768:1024].rearrange("p (h b) -> p h b", h=16), ALU.mult, ALU.mult, [cacc[1], pv, UW],
              [moT.k(i) for i in range(16)])
        for lg in range(4):
            def cy(c, ps, m):
                K.act(YG[:, lg * 4 + c, :], ps[:, :n], AF.Gelu_apprx_tanh, [ps], [f1[3]])
            proj(ab_w_in, 0, 32, 10272 + lg * 512, 512, hT, n, cy)

            def cx(c, ps, m):
                nb_ = lg * 4 + c
                p2 = nb_ % 2
                a_ = conv_s(ps, lpcv(nb_), lxn, nb_, "lru_conv_w", 16, pvc("lru_conv_b", nb_))
                K.dma("sp", waf[p2][:, 0, :], lru_w_a[nb_], [], [waf[p2]], "waf%d" % p2)
                K.dma("sp", waf[p2][:, 1, :], lru_w_i[nb_], [], [waf[p2]], "waf%d" % p2)
                K.cp(wab[p2][:], waf[p2][:, 0, :], [waf[p2]], [wab[p2]], eng="pool")
                K.cp(wib[p2][:], waf[p2][:, 1, :], [waf[p2]], [wib[p2]], eng="pool")
                xb = sq[1]
                K.cp(xb[:, :n], a_[:, :n], [a_], [xb])
                K.mm(GP[0][:, :n], wab[p2][:], xb[:, :n], [wab[p2], xb], [GP[0]])
                K.mm(GP[1][:, :n], wib[p2][:], xb[:, :n], [wib[p2], xb], [GP[1]])
                ga, gi, aa, bb = egB, Rt, X2, XT2
                gaf, gif, aaf, bbf = flat(ga), flat(gi), flat(aa), flat(bb)
                K.act(gaf[:, :n], GP[0][:, :n], AF.Sigmoid, [GP[0], pv], [ga], bias=pvc("lru_b_a", nb_))
                K.act(gif[:, :n], GP[1][:, :n], AF.Sigmoid, [GP[1], pv], [gi], bias=pvc("lru_b_i", nb_))
                K.act(aaf[:, :n], gaf[:, :n], AF.Exp, [ga, c1], [aa], scale=c1[:, nb_:nb_ + 1])
                K.tt(bbf[:, :n], aaf[:, :n], aaf[:, :n], ALU.mult, [aa], [bb])
                K.act(bbf[:, :n], bbf[:, :n], AF.Sqrt, [bb], [bb], scale=-1.0, bias=1.0)
                K.tt(gif[:, :n], gif[:, :n], a_[:, :n], ALU.mult, [gi, a_], [gi])
                K.tt(bbf[:, :n], bbf[:, :n], gif[:, :n], ALU.mult, [bb, gi], [bb])
                K.tt(gaf[:, :n], aaf[:, :n], H0[:, nb_, :], ALU.mult, [aa, DTs], [ga])
                K.tt(HN[:, nb_, :], gaf[:, :n], bbf[:, :n], ALU.add, [ga, bb], [DTc])
                K.tt(moT[:, 16 + nb_, :n], HN[:, nb_, :], YG[:, nb_, :], ALU.mult, [DTc, f1[3]], [moT.k(16 + nb_)])
            proj(ab_w_in, 0, 32, 8224 + lg * 512, 512, hT, n, cx)
        K.dma("sp", oslc.rearrange("(c p) t b -> p c t b", p=128)[:, :, 2, :], lxn[:], [lxn], [], "so2")
        K.dma("sp", osl, HN, [DTc], [], "so2")
        proj(ab_w_out, 0, 32, 0, 4096, moT, n, add_resid(n))

    def mem_attend_sample(l):
        rmsnorm(xres, 32, n, "norm_mem", 32 * l, hT)
        QS = f1[0][:, 0:64]

        def cq(c, ps, m):
            K.ts(QS[:, c * 16:(c + 1) * 16], ps[:, :n], 128.0 ** -0.5, None, ALU.mult, [ps], [f1[0]])
        proj(mem_w_q[l], 0, 32, 0, 512, hT, n, cq)
        for c in range(4):
            K.tr(GP[2][0:16, c * 128:(c + 1) * 128], QS[:, c * 16:(c + 1) * 16], ident, [f1[0], cs], [GP[2]])
        K.cp(Pm[0:16, :, :], v4(GP[2][0:16, :]), [GP[2]], [Pm])
        for b in range(TS):
            Kt = flat(UW).rearrange("p (mc x) -> p mc x", mc=2)
            Vt = flat(Rk).rearrange("p (mc x) -> p mc x", mc=2)
            K.dma("sp", Kt, cmk[l, b].rearrange("(mc p) x -> p mc x", p=128), [], [UW], "cmk")
            K.dma("sp", Vt, cmv[l, b].rearrange("(mc p) x -> p mc x", p=128), [], [Rk], "cmv")
            row_bcast(PB[0], Pm, PTm, b, [])
            sc = esm
            for mc in range(2):
                K.tt(flat(TA), Kt[:, mc, :], PB[0][:, :], ALU.mult, [UW, PB[0]], [TA])
                K.op("dve", "reduce_sum", [TA], [esm], out=esm[:, mc * 4:(mc + 1) * 4], in_=TA[:], axis=AX)
            K.act(esm[:, 0:8], esm[:, 0:8], AF.Exp, [esm], [esm])
            for mc in range(2):
                K.mm(GP[0][:, 0:4], onesf, esm[:, mc * 4:(mc + 1) * 4], [cs, esm], [GP[0]], start=(mc == 0),
                     stop=(mc == 1))
            K.op("dve", "reciprocal", [GP[0]], [c1s], out=c1s[:, 0:4], in_=GP[0][:, 0:4])
            K.tt(esm[:, 0:8].rearrange("p (mc h) -> p mc h", mc=2), esm[:, 0:8].rearrange("p (mc h) -> p mc h", mc=2),
                 c1s[:, 0:4].unsqueeze(1).broadcast_to([128, 2, 4]), ALU.mult, [esm, c1s], [esm])
            for h in range(4):
                for mc in range(2):
                    K.mm(GP[1][:, h * 16 + b:h * 16 + b + 1], Vt[:, mc, h * 128:(h + 1) * 128],
                         esm[:, mc * 4 + h:mc * 4 + h + 1], [Rk, esm], [GP[1]], start=(mc == 0), stop=(mc == 1))
        K.cp(gC[:, :, 0:16], GP[1][:, 0:64].rearrange("p (h b) -> p h b", h=4), [GP[1]], [gC])
        proj(mem_w_o[l], 0, 4, 0, 4096, gC, n, add_resid(n))

    def c_mixer_sample():
        rmsnorm(xres, 32, n, "norm_mix", 32, hT)
        Qf = f1[0].t[:, :].rearrange("p (h b) -> p h b", h=32)
        Ff = f1[1].t[:, :].rearrange("p (h b) -> p h b", h=32)
        Vf = f1[2].t[:, :].rearrange("p (h b) -> p h b", h=32)
        Gf = f1[3].t[:, :].rearrange("p (h b) -> p h b", h=32)
        OH = flat(egB)
        for hg in range(8):
            def cq(c, ps, m):
                h = hg * 4 + c
                K.act(Qf[:, h, :], ps[:, :n], AF.Silu, [ps], [f1[0]])
                K.ts(Qf[:, h, :], Qf[:, h, :], 128.0 ** -0.5, None, ALU.mult, [f1[0]], [f1[0]])
            proj(c_w_in, 0, 32, hg * 512, 512, hT, n, cq)

            def cf(c, ps, m):
                h = hg * 4 + c
                K.act(Ff[:, h, :], ps[:, :n], AF.Sigmoid, [ps], [f1[1]])
                K.ts(Ff[:, h, :], Ff[:, h, :], oml[:, h:h + 1], lb[:, h:h + 1], ALU.mult, [f1[1], oml, lb], [f1[1]],
                     op1=ALU.add)
            proj(c_w_in, 0, 32, 4096 + hg * 512, 512, hT, n, cf)

            def ci_(c, ps, m):
                K.cp(Vf[:, hg * 4 + c, :], ps[:, :n], [ps], [f1[2]], eng="act")
            proj(c_w_in, 0, 32, 8192 + hg * 512, 512, hT, n, ci_)

            def cgz(c, ps, m):
                K.act(Gf[:, hg * 4 + c, :], ps[:, :n], AF.Silu, [ps], [f1[3]])
            proj(c_w_in, 0, 32, 12288 + hg * 512, 512, hT, n, cgz)
            h0 = hg * 4
            for hh in range(4):
                K.tr(GP[2][0:16, hh * 128:(hh + 1) * 128], Ff[:, h0 + hh, :], ident, [f1[1], cs], [GP[2]])
                K.tr(GP[0][0:16, hh * 128:(hh + 1) * 128], Qf[:, h0 + hh, :], ident, [f1[0], cs], [GP[0]])
            K.cp(Pm[0:16, :, :], v4(GP[2][0:16, :]), [GP[2]], [Pm])
            K.cp(PTm[0:16, :, :], v4(GP[0][0:16, :]), [GP[0]], [PTm])
            for b in range(TS):
                bufk = [XR.k(i) for i in (range(0, 4) if b % 2 == 0 else range(4, 8))]
                r0 = 0 if b % 2 == 0 else 4
                ST4 = XR[:, r0:r0 + 4, 0:128]
                K.dma("sp", ST4, sh[b, h0:h0 + 4].rearrange("h v k -> v h k"), [], bufk, "stg%d" % (b % 2))
                row_bcast(PB[0], Pm, X2, b, [])
                row_bcast(PB[1], PTm, XT2, b, [])
                fBv, qBv = v4(PB[0][:, :]), v4(PB[1][:, :])
                vb = Vf[:, h0:h0 + 4, b].unsqueeze(2).broadcast_to([128, 4, 128])
                K.tt(TA[:], ST4, fBv, ALU.mult, bufk + [PB[0]], [TA])
                K.tt(DTs[:], fBv, vb, ALU.mult, [PB[0], f1[2]], [DTs])
                K.tt(TA[:], TA[:], DTs[:], ALU.subtract, [TA, DTs], [TA])
                K.tt(ST4, TA[:], vb, ALU.add, [TA, f1[2]], bufk)
                K.tt(TA[:], ST4, qBv, ALU.mult, bufk + [PB[1]], [TA])
                K.op("dve", "reduce_sum", [TA], [egB], out=OH.rearrange("p (h b) -> p h b", h=32)[:, h0:h0 + 4, b],
                     in_=TA[:], axis=AX)
                K.dma("sp", osh[b, h0:h0 + 4].rearrange("h v k -> v h k"), ST4, bufk, [], "sto%d" % (b % 2))
        K.act(sq[0][:, :], OH, AF.Square, [egB], [sq[0]])
        K.mm(GP[0][:, :], ones_bf[:], sq[0][:, :], [ones_bf, sq[0]], [GP[0]])
        K.act(rstd[:, :], GP[0][:, :], AF.Sqrt, [GP[0]], [rstd], scale=1.0 / 128, bias=EPS)
        K.op("dve", "reciprocal", [rstd], [rstd], out=rstd[:, :], in_=rstd[:, :])
        K.tt(flat(TA), OH, rstd[:, :], ALU.mult, [egB, rstd], [TA])
        K.stt(moT[:, :, :], flat(TA).rearrange("p (h b) -> p h b", h=32), pvc("hgrn_norm_w"), Gf, ALU.mult, ALU.mult,
              [TA, pv, f1[3]], [moT])
        proj(c_w_out, 0, 32, 0, 4096, moT, n, add_resid(n))

    c1s = K.sb("c1s", [128, 4], F32)
    ab_mixer_sample()
    mem_attend_sample(0)
    ffn(0, n)
    c_mixer_sample()
    mem_attend_sample(1)
    ffn(1, n)
    ps = GP[0]
    for kc in range(32):
        s = sq[kc % 2]
        K.act(s[:, :n], xres[:, kc, :n], AF.Square, [xres.k(kc)], [s])
        K.mm(ps[:, :n], ones_bf[:], s[:, :n], [s, ones_bf], [ps], start=(kc == 0), stop=(kc == 31))
    K.act(rstd[:, :n], ps[:, :n], AF.Sqrt, [ps], [rstd], scale=1.0 / 4096, bias=EPS)
    K.op("dve", "reciprocal", [rstd], [rstd], out=rstd[:, :n], in_=rstd[:, :n])
    for kc in range(32):
        K.stt(xres[:, kc, :n], xres[:, kc, :n], pvc("norm_final", kc), rstd[:, :n], ALU.mult, ALU.mult,
              [xres.k(kc), rstd, pv], [xres.k(kc)])
    K.dma("sp", ysT.rearrange("(kc p) t -> p kc t", p=128), xres[:, :, :], [xres], [], "yout")
    return K.finalize()


_NC_CACHE = {}


def _fm(v):
    v = np.asarray(v, np.float32)
    return np.ascontiguousarray(v.reshape(-1, 128).T)


def kernel(**inp):
    NT = 2048 // NTILE
    if NT not in _NC_CACHE:
        _NC_CACHE[NT] = build(NT)
    nc = _NC_CACHE[NT]
    f = lambda a: np.ascontiguousarray(np.asarray(a, np.float32))
    pvh = np.zeros((128, PVN), np.float32)

    def put(name, cols):
        pvh[:, PV[name]:PV[name] + cols.shape[1]] = cols
    put("norm_mix", _fm(inp["norm_mix"])); put("norm_mem", _fm(inp["norm_mem"]))
    put("norm_mem_kv", _fm(inp["norm_mem_kv"])); put("norm_ffn", _fm(inp["norm_ffn"]))
    put("norm_final", _fm(inp["norm_final"])); put("gdn_conv_w", _fm(inp["gdn_conv_w"][0]))
    put("lru_conv_w", _fm(inp["lru_conv_w"][0])); put("lru_conv_b", _fm(inp["lru_conv_b"][0]))
    put("lru_b_a", _fm(inp["lru_b_a"][0])); put("lru_b_i", _fm(inp["lru_b_i"][0]))
    put("lru_lam", _fm(inp["lru_lam"][0])); put("lb_raw", _fm(inp["hgrn_lb_raw"]))
    put("gdn_norm_w", _fm(inp["gdn_norm_w"][0])); put("hgrn_norm_w", _fm(inp["hgrn_norm_w"][0]))
    pvh[0:16, PV["a_log"]] = np.asarray(inp["gdn_a_log"][0], np.float32)
    pvh[0:16, PV["dt_bias"]] = np.asarray(inp["gdn_dt_bias"][0], np.float32)
    consts = make_consts()
    shared = dict(pv=pvh, consts=consts, ab_w_in=f(inp["ab_w_in"][0]), ab_w_out=f(inp["ab_w_out"][0]),
                  c_w_in=f(inp["c_w_in"][0]), c_w_out=f(inp["c_w_out"][0]), mem_w_q=f(inp["mem_w_q"]),
                  mem_w_k=f(inp["mem_w_k"]), mem_w_v=f(inp["mem_w_v"]), mem_w_o=f(inp["mem_w_o"]),
                  ffn_w_up=f(inp["ffn_w_up"]), ffn_w_down=f(inp["ffn_w_down"]), lru_w_a=f(inp["lru_w_a"][0]),
                  lru_w_i=f(inp["lru_w_i"][0]))
    in_maps = []
    zx = np.zeros((4096, 2048), np.float32)
    zm = np.zeros((4096, 256), np.float32)
    for c in range(NCORES):
        b0 = c * TS
        m = dict(shared)
        if c % 2 == 0:
            m["xT"] = f(np.asarray(inp["x_prompt"][c // 2]).T)
            m["memT"] = f(np.asarray(inp["mem_prompt"][c // 2]).T)
        else:
            m["xT"] = zx
            m["memT"] = zm
        m["xsT"] = f(np.asarray(inp["x_sample"][b0:b0 + TS, 0, :]).T)
        m["cmk"] = f(np.asarray(inp["cache_mem_k"][:, b0:b0 + TS]).reshape(2, TS, 256, 512))
        m["cmv"] = f(np.asarray(inp["cache_mem_v"][:, b0:b0 + TS]).reshape(2, TS, 256, 512))
        m["sgc"] = f(np.asarray(inp["state_gdn_conv"][0, b0:b0 + TS]).transpose(2, 1, 0))
        m["sg"] = f(np.asarray(inp["state_gdn"][0, b0:b0 + TS]).transpose(0, 1, 3, 2))
        m["slc"] = f(np.asarray(inp["state_lru_conv"][0, b0:b0 + TS]).transpose(2, 1, 0))
        m["sl"] = f(np.asarray(inp["state_lru"][0, b0:b0 + TS]).T)
        m["sh"] = f(np.asarray(inp["state_hgrn"][0, b0:b0 + TS]).transpose(0, 1, 3, 2))
        in_maps.append(m)
    res = run_bass_kernel_spmd(nc, in_maps, core_ids=list(range(NCORES)))
    R = res.results
    R4 = [R[0], R[2], R[4], R[6]]
    y_prompt = np.stack([R4[s]["yT"].T for s in range(4)])
    y_sample = np.concatenate([R[c]["ysT"].T for c in range(NCORES)])[:, None, :]
    pk = np.stack([np.stack([R4[s]["okT"][l].T.reshape(256, 4, 128) for s in range(4)]) for l in range(2)])
    pvv = np.stack([np.stack([R4[s]["ov"][l].reshape(256, 4, 128) for s in range(4)]) for l in range(2)])
    p_gc = np.stack([R4[s]["ogc"].reshape(128, 48, 3).transpose(2, 1, 0).reshape(3, 6144) for s in range(4)])[None]
    p_g = np.stack([R4[s]["og"] for s in range(4)])[None]
    p_lc = np.stack([R4[s]["olc"].reshape(128, 16, 3).transpose(2, 1, 0).reshape(3, 2048) for s in range(4)])[None]
    p_l = np.stack([R4[s]["ol"].T.reshape(2048) for s in range(4)])[None]
    p_h = np.stack([R4[s]["oh"] for s in range(4)])[None]
    s_gc = np.concatenate([R[c]["osgc"].transpose(2, 1, 0) for c in range(NCORES)])[None]
    s_g = np.concatenate([R[c]["osg"].transpose(0, 1, 3, 2) for c in range(NCORES)])[None]
    s_lc = np.concatenate([R[c]["oslc"].transpose(2, 1, 0) for c in range(NCORES)])[None]
    s_l = np.concatenate([R[c]["osl"].transpose(2, 1, 0).reshape(TS, 2048) for c in range(NCORES)])[None]
    s_h = np.concatenate([R[c]["osh"].transpose(0, 1, 3, 2) for c in range(NCORES)])[None]
    outs = (y_prompt, y_sample, pk, pvv, p_gc, p_g, p_lc, p_l, p_h, s_gc, s_g, s_lc, s_l, s_h)
    return tuple(np.ascontiguousarray(o, dtype=np.float32) for o in outs)
```

```python
import numpy as np
import concourse.bass as bass
import concourse.mybir as mybir
from concourse.bass_utils import run_bass_kernel_spmd

F32 = mybir.dt.float32
BF16 = mybir.dt.bfloat16
AF = mybir.ActivationFunctionType
ALU = mybir.AluOpType
EPS = 1e-6
NCORES = 8
TS = 16
NTILE = 256


class Tile:
    def __init__(self, t, name, nsub=1):
        self.t, self.name, self.nsub = t, name, nsub

    def __getitem__(self, idx):
        return self.t[idx]

    def k(self, i):
        return (self.name, i)

    def keys(self):
        return [(self.name, i) for i in range(self.nsub)]


def _keys(lst):
    out = []
    for x in lst:
        if isinstance(x, Tile):
            out.extend(x.keys())
        else:
            out.append(x)
    return out


class Ker:
    def __init__(self):
        self.nc = bass.Bass("TRN2", target_bir_lowering=False)
        self.ops = []
        self.nsl = 0

    def sb(self, name, shape, dt, nsub=1):
        return Tile(self.nc.alloc_sbuf_tensor(name, list(shape), dt), name, nsub)

    def ps(self, name, shape, dt, nsub=1):
        if not hasattr(self, "psn"):
            self.psn = set()
        self.psn.add(name)
        return Tile(self.nc.alloc_psum_tensor(name, list(shape), dt), name, nsub)

    def op(self, eng, meth, R, W, *args, **kw):
        Rk, Wk = _keys(R), _keys(W)
        psn = getattr(self, "psn", ())
        Wk = Wk + [k for k in Rk if k[0] in psn and k not in Wk]
        self.ops.append(dict(eng=eng, meth=meth, R=Rk, W=Wk, args=args, kw=kw, dma=None))

    def dma(self, q, out, in_, R, W, key):
        self.ops.append(dict(eng=q, meth="dma_start", R=_keys(R), W=_keys(W), args=(), kw=dict(out=out, in_=in_),
                             dma=key))

    def act(self, out, in_, func, R, W, **kw):
        self.op("act", "activation", R, W, out=out, in_=in_, func=func, **kw)

    def mm(self, out, lhsT, rhs, R, W, start=True, stop=True):
        self.op("pe", "matmul", R, W, out, lhsT=lhsT, rhs=rhs, start=start, stop=stop)

    def tr(self, out, in_, ident, R, W):
        self.op("pe", "transpose", R, W, out, in_, ident)

    def tt(self, out, in0, in1, op, R, W, eng="dve"):
        self.op(eng, "tensor_tensor", R, W, out=out, in0=in0, in1=in1, op=op)

    def ts(self, out, in0, s1, s2, op0, R, W, op1=None, eng="dve", **kw):
        if op1 is None:
            self.op(eng, "tensor_scalar", R, W, out=out, in0=in0, scalar1=s1, scalar2=None, op0=op0, **kw)
        else:
            self.op(eng, "tensor_scalar", R, W, out=out, in0=in0, scalar1=s1, scalar2=s2, op0=op0, op1=op1, **kw)

    def stt(self, out, in0, scalar, in1, op0, op1, R, W, eng="dve", **kw):
        self.op(eng, "scalar_tensor_tensor", R, W, out=out, in0=in0, scalar=scalar, in1=in1, op0=op0, op1=op1, **kw)

    def cp(self, out, in_, R, W, eng="dve"):
        if eng == "act":
            self.op("act", "activation", R, W, out=out, in_=in_, func=AF.Copy)
        else:
            self.op(eng, "tensor_copy", R, W, out=out, in_=in_)

    def finalize(self):
        nc = self.nc
        engs = {"pe": nc.tensor, "act": nc.scalar, "dve": nc.vector, "pool": nc.gpsimd, "sp": nc.sync}
        ops = self.ops
        lastw = {}
        readers = {}
        dmacnt = {}
        for i, o in enumerate(ops):
            deps = set()
            for r in o["R"]:
                if r in lastw:
                    deps.add(lastw[r])
            for w in o["W"]:
                if w in lastw:
                    deps.add(lastw[w])
                for rr in readers.get(w, ()):
                    deps.add(rr)
            deps.discard(i)
            need = []
            for j in deps:
                pj = ops[j]
                if pj["dma"] is not None:
                    need.append(("dma", pj["dma"], dmacnt[pj["dma"]]))
                elif pj["eng"] == o["eng"] and o["dma"] is None and o["eng"] == "pe":
                    continue
                else:
                    pj["sig"] = True
                    need.append(("eng", pj["eng"], j))
            o["need"] = need
            for w in o["W"]:
                lastw[w] = i
                readers[w] = []
            for r in o["R"]:
                readers.setdefault(r, []).append(i)
            if o["dma"] is not None:
                dmacnt[o["dma"]] = dmacnt.get(o["dma"], 0) + 16
                o["dmaval"] = dmacnt[o["dma"]]
        cnt = {}
        for o in ops:
            if o.get("sig"):
                cnt[o["eng"]] = cnt.get(o["eng"], 0) + 1
                o["sigval"] = cnt[o["eng"]]
        esem = {e: nc.alloc_semaphore("sem_" + e) for e in ("pe", "act", "dve", "pool", "sp")}
        dsem = {}
        waited = {}
        last_dma = {}
        for o in ops:
            e = engs[o["eng"]]
            for kind, a, b in o["need"]:
                if kind == "dma":
                    if a not in dsem:
                        dsem[a] = nc.alloc_semaphore("d_" + a)
                    sem, val, sk = dsem[a], b, ("d", a)
                else:
                    sem, val, sk = esem[a], ops[b]["sigval"], ("e", a)
                wk = (o["eng"], sk)
                if waited.get(wk, 0) >= val:
                    continue
                waited[wk] = val
                e.wait_ge(sem, val)
            ins = getattr(e, o["meth"])(*o["args"], **o["kw"])
            if o["dma"] is not None:
                if o["dma"] not in dsem:
                    dsem[o["dma"]] = nc.alloc_semaphore("d_" + o["dma"])
                ins.then_inc(dsem[o["dma"]], 16)
                last_dma[o["dma"]] = o["dmaval"]
            elif o.get("sig"):
                ins.then_inc(esem[o["eng"]], 1)
        for key, val in last_dma.items():
            nc.sync.wait_ge(dsem[key], val)
        return nc


PV = {}
_c = 0
for _n, _w in [("norm_mix", 64), ("norm_mem", 64), ("norm_mem_kv", 64), ("norm_ffn", 64), ("norm_final", 32),
               ("gdn_conv_w", 192), ("lru_conv_w", 64), ("lru_conv_b", 16), ("lru_b_a", 16), ("lru_b_i", 16),
               ("lru_lam", 16), ("lb_raw", 64), ("gdn_norm_w", 1), ("hgrn_norm_w", 1), ("a_log", 1),
               ("dt_bias", 1)]:
    PV[_n] = _c
    _c += _w
PVN = _c
CI, CO, CUC, CUS, CLS, CHM, CRM, CSEL = 0, 128, 256, 384, 512, 640, 768, 1280
CN = 1280


def make_consts():
    c = np.zeros((128, CN), np.float32)
    p = np.arange(128)[:, None]
    i = np.arange(128)[None, :]
    c[:, CI:CI + 128] = (p == i)
    c[:, CO:CO + 128] = 1.0
    c[:, CUC:CUC + 128] = (p <= i)
    c[:, CUS:CUS + 128] = (p < i)
    c[:, CLS:CLS + 128] = (p > i)
    c[:, CHM:CHM + 128] = (p <= i) & ((p // 64) == (i // 64))
    rm = np.ones(512, np.float32)
    rm[::64] = 0.0
    c[:, CRM:CRM + 512] = rm[None, :]
    return c


def build(NT):
    K = Ker()
    nc = K.nc
    N = NTILE
    L = NT * N

    def din(name, shape):
        return nc.dram_tensor(name, list(shape), F32, kind="ExternalInput").ap()

    def dout(name, shape):
        return nc.dram_tensor(name, list(shape), F32, kind="ExternalOutput").ap()

    xT = din("xT", [4096, L]); xsT = din("xsT", [4096, TS]); memT = din("memT", [4096, 256])
    cmk = din("cmk", [2, TS, 256, 512]); cmv = din("cmv", [2, TS, 256, 512])
    sgc = din("sgc", [6144, 3, TS]); sg = din("sg", [TS, 16, 128, 128])
    slc = din("slc", [2048, 3, TS]); sl = din("sl", [2048, TS]); sh = din("sh", [TS, 32, 128, 128])
    pvd = din("pv", [128, PVN]); cst = din("consts", [128, CN])
    ab_w_in = din("ab_w_in", [4096, 12320]); ab_w_out = din("ab_w_out", [4096, 4096])
    c_w_in = din("c_w_in", [4096, 16384]); c_w_out = din("c_w_out", [4096, 4096])
    mem_w_q = din("mem_w_q", [2, 4096, 512]); mem_w_k = din("mem_w_k", [2, 4096, 512])
    mem_w_v = din("mem_w_v", [2, 4096, 512]); mem_w_o = din("mem_w_o", [2, 512, 4096])
    ffn_w_up = din("ffn_w_up", [2, 4096, 16384]); ffn_w_down = din("ffn_w_down", [2, 16384, 4096])
    lru_w_a = din("lru_w_a", [16, 128, 128]); lru_w_i = din("lru_w_i", [16, 128, 128])

    yT = dout("yT", [4096, L]); ysT = dout("ysT", [4096, TS])
    okT = dout("okT", [2, 512, 256]); ov = dout("ov", [2, 256, 512])
    ogc = dout("ogc", [128, 144]); og = dout("og", [16, 128, 128]); olc = dout("olc", [128, 48])
    ol = dout("ol", [128, 16]); oh = dout("oh", [32, 128, 128])
    osgc = dout("osgc", [6144, 3, TS]); osg = dout("osg", [TS, 16, 128, 128])
    oslc = dout("oslc", [2048, 3, TS]); osl = dout("osl", [128, 16, TS]); osh = dout("osh", [TS, 32, 128, 128])

    xres = K.sb("xres", [128, 32, N], F32, nsub=32)
    hT = K.sb("hT", [128, 32, N], BF16)
    moT = K.sb("moT", [128, 32, N], BF16, nsub=32)
    NSL = 5
    wsl = [K.sb(f"wsl{i}", [128, 4, 512], BF16) for i in range(NSL)]
    pv = K.sb("pvs", [128, PVN], F32)
    cs = K.sb("cs", [128, CN], F32)
    ones_bf = K.sb("ones_bf", [128, 128], BF16)
    ident_bf = K.sb("ident_bf", [128, 128], BF16)
    hm_bf = K.sb("hm_bf", [128, 128], BF16)
    sq = [K.sb(f"sq{i}", [128, 512], BF16) for i in range(2)]
    rstd = K.sb("rstd", [128, 512], F32)
    xp = [K.sb(f"xp{i}", [128, N + 3], F32) for i in range(2)]
    cacc = [K.sb(f"cacc{i}", [128, N], F32) for i in range(2)]
    gA = K.sb("gA", [128, 4, N], BF16, nsub=4)
    gB = K.sb("gB", [128, 4, N], BF16, nsub=4)
    gC = K.sb("gC", [128, 4, N], BF16, nsub=4)
    gD = K.sb("gD", [128, 4, N], BF16, nsub=4)
    gE = K.sb("gE", [128, 4, N], BF16, nsub=4)
    f1 = [K.sb(f"f1_{i}", [128, 512], F32) for i in range(4)]
    convst = K.sb("convst", [128, 48, 3], F32)
    lconvst = K.sb("lconvst", [128, 16, 3], F32)
    hst = K.sb("hst", [128, 16], F32)
    Sg = K.sb("Sg", [128, 16, 128], F32, nsub=16)
    Sgb = K.sb("Sgb", [128, 4, 128], BF16, nsub=4)
    Sh = K.sb("Sh", [128, 32, 128], F32, nsub=32)
    Shb = K.sb("Shb", [128, 128], BF16)
    Shb2 = K.sb("Shb2", [128, 128], BF16)
    KTb = [K.sb(f"KTb{l}", [128, 4, 256], BF16) for l in range(2)]
    Vb = [K.sb(f"Vb{l}", [128, 2, 512], BF16) for l in range(2)]
    stg = K.sb("stg", [128, 4, 256], F32)
    gbT = K.sb("gbT", [16, 2, N], F32)
    gtok = K.sb("gtok", [128, 4, 16], F32)
    btok = K.sb("btok", [128, 4, 16], F32)
    nbtok = K.sb("nbtok", [128, 4, 16], F32)
    esm = K.sb("esm", [128, 8], F32)
    c1 = K.sb("c1", [128, 16], F32)
    nalog = K.sb("nalog", [16, 1], F32)
    lb = K.sb("lb", [128, 32], F32)
    oml = K.sb("oml", [128, 32], F32)
    wab = [K.sb(f"wab{i}", [128, 128], BF16) for i in range(2)]
    wib = [K.sb(f"wib{i}", [128, 128], BF16) for i in range(2)]
    waf = [K.sb(f"waf{i}", [128, 2, 128], F32) for i in range(2)]
    def g4(name, dt=F32):
        return K.sb(name, [128, 4, 128], dt)
    Rt, DTs, DTc, egB, Pm, PTm, X2, XT2, TA = [g4(n) for n in
                                               ("Rt", "DTs", "DTc", "egB", "Pm", "PTm", "X2", "XT2", "TA")]
    atT, kdt, wTb, qdc, vnb, osq = [g4(n, BF16) for n in ("atT", "kdt", "wTb", "qdc", "vnb", "osq")]
    Rk = K.sb("Rk", [128, 4, 256], F32)
    UW = K.sb("UW", [128, 4, 256], F32)

    PB = [K.ps(f"PB{i}", [128, 512], F32) for i in range(4)]
    GP = [K.ps(f"GP{i}", [128, 512], F32) for i in range(3)]
    TB = K.ps("TB", [128, 1024], BF16)

    ident = cs[:, CI:CI + 128]
    onesf = cs[:, CO:CO + 128]
    Ucaus = cs[:, CUC:CUC + 128]
    Ustr = cs[:, CUS:CUS + 128]
    Lstr = cs[:, CLS:CLS + 128]

    def b4(ap2d):
        return ap2d.unsqueeze(1).broadcast_to([128, 4, 128])

    def pvc(name, i=0, w=1):
        return pv[:, PV[name] + i:PV[name] + i + w]

    K.dma("sp", pv[:], pvd, [], [pv], "pv")
    K.dma("sp", cs[:], cst, [], [cs], "cs")
    K.cp(ones_bf[:], onesf, [cs], [ones_bf])
    K.cp(ident_bf[:], ident, [cs], [ident_bf])
    K.cp(hm_bf[:], cs[:, CHM:CHM + 128], [cs], [hm_bf])
    K.act(c1[:], pvc("lru_lam", 0, 16), AF.Exp, [pv], [c1], scale=-1.0)
    K.act(c1[:], c1[:], AF.Ln, [c1], [c1], bias=1.0)
    K.ts(c1[:], c1[:], -8.0, None, ALU.mult, [c1], [c1])
    K.act(nalog[:], pv[0:16, PV["a_log"]:PV["a_log"] + 1], AF.Exp, [pv], [nalog])
    K.ts(nalog[:], nalog[:], -1.0, None, ALU.mult, [nalog], [nalog])
    K.tt(lb[:], pvc("lb_raw", 32, 32), pvc("lb_raw", 0, 32), ALU.subtract, [pv], [lb])
    K.act(lb[:], lb[:], AF.Sigmoid, [lb], [lb])
    K.ts(oml[:], lb[:], -1.0, 1.0, ALU.mult, [lb], [oml], op1=ALU.add)

    K.op("dve", "memset", [], [Sg], Sg[:, :, :], 0.0)
    K.op("dve", "memset", [], [Sh], Sh[:, :, :], 0.0)
    def rmsnorm(X, nk, n, wname, woff, dst):
        ps = GP[0]
        for kc in range(nk):
            s = sq[kc % 2]
            K.act(s[:, :n], X[:, kc, :n], AF.Square, [X.k(kc)], [s])
            K.mm(ps[:, :n], ones_bf[:], s[:, :n], [s, ones_bf], [ps], start=(kc == 0), stop=(kc == nk - 1))
        K.act(rstd[:, :n], ps[:, :n], AF.Sqrt, [ps], [rstd], scale=1.0 / (nk * 128), bias=EPS)
        K.op("dve", "reciprocal", [rstd], [rstd], out=rstd[:, :n], in_=rstd[:, :n])
        for kc in range(nk):
            K.stt(dst[:, kc, :n], X[:, kc, :n], pvc(wname, woff + kc), rstd[:, :n], ALU.mult, ALU.mult,
                  [X.k(kc), rstd, pv], [dst])

    scr = {}
    scr_off = [0]
    SCR_CH = 120 * 1024 * 1024
    wscrs = [nc.dram_tensor("wscr%d" % i, [SCR_CH], BF16, kind="Internal").ap() for i in range(4)]

    def proj(Wd, row0, nk, col0, ncols, rhs, n, consumer, rkeys=None, gcons=None):
        rk = [rhs] if rkeys is None else rkeys
        for g0 in range(0, ncols, 512):
            gcn = min(512, ncols - g0)
            nch = (gcn + 127) // 128
            nkt = (nk + 3) // 4
            for kt in range(nkt):
                kk = min(4, nk - kt * 4)
                slot = wsl[K.nsl % NSL]
                K.nsl += 1
                src = Wd[row0 + kt * 512:row0 + kt * 512 + kk * 128, col0 + g0:col0 + g0 + gcn].rearrange(
                    "(kc p) n -> p kc n", p=128)
                wkey = (str(Wd), row0 + kt * 512, col0 + g0, kk, gcn)
                if wkey not in scr:
                    sz_ = 128 * kk * gcn
                    if (scr_off[0] % SCR_CH) + sz_ > SCR_CH:
                        scr_off[0] = (scr_off[0] // SCR_CH + 1) * SCR_CH
                    off = scr_off[0]
                    scr_off[0] += sz_
                    ci_ = len(scr)
                    scr[wkey] = (off, ci_)
                    wscr = wscrs[off // SCR_CH]
                    o_ = off % SCR_CH
                    dstv = wscr[o_:o_ + sz_].rearrange("(p k n) -> p k n", p=128, k=kk)
                    K.dma("pool", dstv, src, [], [("scr", ci_), ("thr", ci_ % 6)], "scrc%d" % (ci_ % 6))
                off, ci_ = scr[wkey]
                wscr = wscrs[off // SCR_CH]
                o_ = off % SCR_CH
                srcv = wscr[o_:o_ + 128 * kk * gcn].rearrange("(p k n) -> p k n", p=128, k=kk)
                K.dma("sp", slot[:, :kk, :gcn], srcv, [("scr", ci_), ("thr", ci_ % 6)], [slot], slot.name)
                for c in range(nch):
                    m = min(128, gcn - c * 128)
                    for k in range(kk):
                        K.mm(PB[c][:m, :n], slot[:, k, c * 128:c * 128 + m], rhs[:, kt * 4 + k, :n], [slot] + rk,
                             [PB[c]], start=(kt == 0 and k == 0), stop=(kt == nkt - 1 and k == kk - 1))
            if gcons is not None:
                gcons([(g0 // 128 + c, PB[c], min(128, gcn - c * 128)) for c in range(nch)])
            else:
                for c in range(nch):
                    consumer(g0 // 128 + c, PB[c], min(128, gcn - c * 128))

    def add_resid(n):
        def f(c, ps, m):
            K.tt(xres[:, c, :n], xres[:, c, :n], ps[:, :n], ALU.add, [xres.k(c), ps], [xres.k(c)])
        return f

    def mem_project():
        K.dma("sp", xres[:, :, 0:256], memT.rearrange("(kc p) t -> p kc t", p=128), [], [xres], "xin")
        for l in range(2):
            rmsnorm(xres, 32, 256, "norm_mem_kv", 32 * l, hT)

            def ck(c, ps, m, l=l):
                K.cp(stg[:, c, :], ps[:, :256], [ps], [stg], eng="act")
                K.cp(KTb[l][:, c, :], ps[:, :256], [ps], [KTb[l]])
            proj(mem_w_k[l], 0, 32, 0, 512, hT, 256, ck)
            K.dma("sp", okT[l].rearrange("(c p) m -> p c m", p=128), stg[:], [stg], [], "okT")

            def cv(c, ps, m, l=l):
                K.cp(stg[:, c, :], ps[:, :256], [ps], [stg], eng="act")
            proj(mem_w_v[l], 0, 32, 0, 512, hT, 256, cv)
            for mc in range(2):
                for c in range(4):
                    K.tr(GP[mc][:, c * 128:(c + 1) * 128], stg[:, c, mc * 128:(mc + 1) * 128], ident, [stg, cs],
                         [GP[mc]])
                K.cp(Vb[l][:, mc, :], GP[mc][:, :], [GP[mc]], [Vb[l]])
                K.cp(f1[mc][:, :], GP[mc][:, :], [GP[mc]], [f1[mc]], eng="act")
                K.dma("sp", ov[l, mc * 128:(mc + 1) * 128, :], f1[mc][:, :], [f1[mc]], [], "ov")

    def mem_attend_prompt(l, n):
        rmsnorm(xres, 32, n, "norm_mem", 32 * l, hT)

        def cq(c, ps, m):
            K.ts(gA[:, c, :n], ps[:, :n], 128.0 ** -0.5, None, ALU.mult, [ps], [gA.k(c)])
        proj(mem_w_q[l], 0, 32, 0, 512, hT, n, cq)
        for h in range(4):
            for mc in range(2):
                K.mm(GP[mc][:, :n], KTb[l][:, h, mc * 128:(mc + 1) * 128], gA[:, h, :n], [KTb[l], gA.k(h)], [GP[mc]])
                K.act(gB[:, mc, :n], GP[mc][:, :n], AF.Exp, [GP[mc]], [gB.k(mc)])
            for mc in range(2):
                K.mm(GP[2][:, :n], ones_bf[:], gB[:, mc, :n], [ones_bf, gB.k(mc)], [GP[2]], start=(mc == 0),
                     stop=(mc == 1))
            for mc in range(2):
                K.mm(GP[0][:, :n], Vb[l][:, mc, h * 128:(h + 1) * 128], gB[:, mc, :n], [Vb[l], gB.k(mc)], [GP[0]],
                     start=(mc == 0), stop=(mc == 1))
            K.op("dve", "reciprocal", [GP[2]], [f1[0]], out=f1[0][:, :n], in_=GP[2][:, :n])
            K.tt(gC[:, h, :n], GP[0][:, :n], f1[0][:, :n], ALU.mult, [GP[0], f1[0]], [gC.k(h)])
        proj(mem_w_o[l], 0, 4, 0, 4096, gC, n, add_resid(n))

    def ffn(l, n):
        rmsnorm(xres, 32, n, "norm_ffn", 32 * l, hT)
        for g in range(8):
            def cu(c, ps, m):
                cc = c % 16
                t = f1[cc % 4]
                K.act(t[:, :n], ps[:, :n], AF.Relu, [ps], [t])
                K.tt(moT[:, cc, :n], t[:, :n], t[:, :n], ALU.mult, [t], [moT.k(cc)], eng="pool")
            proj(ffn_w_up[l], 0, 32, g * 2048, 2048, hT, n, cu)
            proj(ffn_w_down[l], g * 2048, 16, 0, 4096, moT, n, add_resid(n), rkeys=[moT.k(i) for i in range(16)])

    def conv_fm(ps, n, cst_tile, cidx, wname, wstride, widx, bias_ap, first_tile, par):
        x_ = xp[par]
        a_ = cacc[par]
        if first_tile:
            K.op("dve", "memset", [], [x_], x_[:, 0:3], 0.0)
        else:
            K.cp(x_[:, 0:3], cst_tile[:, cidx, :], [cst_tile], [x_])
        K.cp(x_[:, 3:3 + n], ps[:, :n], [ps], [x_], eng="act")
        K.cp(cst_tile[:, cidx, :], x_[:, n:n + 3], [x_], [cst_tile])
        K.ts(a_[:, :n], x_[:, 0:n], pvc(wname, widx), None, ALU.mult, [x_, pv], [a_])
        for tap in range(1, 4):
            K.stt(a_[:, :n], x_[:, tap:tap + n], pvc(wname, tap * wstride + widx), a_[:, :n], ALU.mult, ALU.add,
                  [x_, pv, a_], [a_])
        if bias_ap is not None:
            K.ts(a_[:, :n], a_[:, :n], bias_ap, None, ALU.add, [a_, pv], [a_])
        return a_

    def l2norm_to(src, dst, n, scale):
        s = sq[0]
        K.act(s[:, :n], src[:, :n], AF.Square, [src], [s])
        K.mm(GP[0][:, :n], ones_bf[:], s[:, :n], [s, ones_bf], [GP[0]])
        K.act(rstd[:, :n], GP[0][:, :n], AF.Sqrt, [GP[0]], [rstd], bias=EPS)
        K.op("dve", "reciprocal", [rstd], [rstd], out=rstd[:, :n], in_=rstd[:, :n])
        K.stt(dst, src[:, :n], scale, rstd[:, :n], ALU.mult, ALU.mult, [src, rstd], [])

    def mkbuf(tile, flatten=False):
        if flatten:
            fa = tile[:].rearrange("p a b -> p (a b)")
            return (tile, lambda lo, hi: fa[:, lo:hi])
        return (tile, lambda lo, hi: tile[:, lo:hi])

    def conv_group(items, n, cst_tile, cids, wname, wstride, bias_name, first, XP, CA):
        for c, (cg, ps, m) in enumerate(items):
            xt, xa = XP[c]
            if first:
                K.op("dve", "memset", [], [xt], xa(0, 3), 0.0)
            else:
                K.cp(xa(0, 3), cst_tile[:, cids[c], :], [cst_tile], [xt])
        for c, (cg, ps, m) in enumerate(items):
            xt, xa = XP[c]
            K.cp(xa(3, 3 + n), ps[:, :n], [ps], [xt], eng="act")
        for c in range(len(items)):
            xt, xa = XP[c]
            K.cp(cst_tile[:, cids[c], :], xa(n, n + 3), [xt], [cst_tile])
        for c in range(len(items)):
            xt, xa = XP[c]
            at, aa = CA[c]
            K.ts(aa(0, n), xa(0, n), pvc(wname, cids[c]), None, ALU.mult, [xt, pv], [at])
        for tap in range(1, 4):
            for c in range(len(items)):
                xt, xa = XP[c]
                at, aa = CA[c]
                K.stt(aa(0, n), xa(tap, tap + n), pvc(wname, tap * wstride + cids[c]), aa(0, n), ALU.mult, ALU.add,
                      [xt, pv, at], [at])
        if bias_name is not None:
            for c in range(len(items)):
                at, aa = CA[c]
                K.ts(aa(0, n), aa(0, n), pvc(bias_name, cids[c]), None, ALU.add, [at, pv], [at])

    def gqkv(items, which, hg, dst, scale, first, n):
        XP = [mkbuf(xp[0]), mkbuf(xp[1]), mkbuf(UW, True), mkbuf(Rk, True)]
        CA = [mkbuf(cacc[0]), mkbuf(cacc[1]), mkbuf(TA, True), mkbuf(X2, True)]
        SQ = [mkbuf(sq[0]), mkbuf(sq[1]), mkbuf(osq, True), mkbuf(vnb, True)]
        RS = [mkbuf(rstd), mkbuf(f1[0]), mkbuf(f1[1]), mkbuf(f1[2])]
        PSN = [(GP[0], 0), (GP[0], 256), (GP[1], 0), (GP[1], 256)]
        cids = [which * 16 + hg * 4 + c for c in range(4)]
        conv_group(items, n, convst, cids, "gdn_conv_w", 48, None, first, XP, CA)
        if scale is None:
            for c in range(4):
                at, aa = CA[c]
                K.act(dst[:, c, :n], aa(0, n), AF.Silu, [at], [dst.k(c)])
            return
        for c in range(4):
            at, aa = CA[c]
            K.act(aa(0, n), aa(0, n), AF.Silu, [at], [at])
        for c in range(4):
            at, aa = CA[c]
            st, sa = SQ[c]
            K.act(sa(0, n), aa(0, n), AF.Square, [at], [st])
        for c in range(4):
            st, sa = SQ[c]
            pt, po = PSN[c]
            K.mm(pt[:, po:po + n], ones_bf[:], sa(0, n), [st, ones_bf], [pt])
        for c in range(4):
            pt, po = PSN[c]
            rt, ra = RS[c]
            K.act(ra(0, n), pt[:, po:po + n], AF.Sqrt, [pt], [rt], bias=EPS)
        for c in range(4):
            rt, ra = RS[c]
            K.op("dve", "reciprocal", [rt], [rt], out=ra(0, n), in_=ra(0, n))
        for c in range(4):
            at, aa = CA[c]
            rt, ra = RS[c]
            K.stt(dst[:, c, :n], aa(0, n), scale, ra(0, n), ALU.mult, ALU.mult, [at, rt], [dst.k(c)])

    def gcf(items, hg, n):
        nch = n // 64
        TA_ = [Rt, DTc, Pm, X2]
        TB_ = [DTs, egB, PTm, XT2]
        FB = [mkbuf(t, True) for t in TA_]
        GB = [mkbuf(t, True) for t in TB_]
        R4 = range(4)
        for c, (cg, ps, m) in enumerate(items):
            K.act(FB[c][1](0, n), ps[:, :n], AF.Sigmoid, [ps], [FB[c][0]])
        for c in R4:
            h = hg * 4 + c
            t, f = FB[c]
            K.ts(f(0, n), f(0, n), oml[:, h:h + 1], lb[:, h:h + 1], ALU.mult, [t, oml, lb], [t], op1=ALU.add)
        for c in R4:
            t, f = FB[c]
            K.act(f(256, 256 + n), f(0, n), AF.Ln, [t], [t])
        for c in R4:
            t, f = FB[c]
            t2, g = GB[c]
            K.op("dve", "tensor_tensor_scan", [t, cs], [t2], out=g(0, n), data0=cs[:, CRM:CRM + n],
                 data1=f(256, 256 + n), initial=0.0, op0=ALU.mult, op1=ALU.add)
        for c in R4:
            t, f = FB[c]
            K.ts(f(0, n), f(0, n), -1.0, 1.0, ALU.mult, [t], [t], op1=ALU.add)
        for c in R4:
            t2, g = GB[c]
            K.act(g(256, 256 + n), g(0, n), AF.Exp, [t2], [t2])
        for c in R4:
            t2, g = GB[c]
            K.stt(gA[:, c, :n], f1[c][:, :n], 128.0 ** -0.5, g(256, 256 + n), ALU.mult, ALU.mult, [f1[c], t2],
                  [gA.k(c)])
        for c in R4:
            t, f = FB[c]
            t2, g = GB[c]
            K.act(f(256, 256 + n), g(0, n), AF.Exp, [t2], [t], scale=-1.0)
        for c in R4:
            t, f = FB[c]
            K.tt(f(256, 256 + n), f(256, 256 + n), f(0, n), ALU.mult, [t], [t])
        for c in R4:
            t, f = FB[c]
            K.cp(gE[:, c, :n], f(256, 256 + n), [t], [gE.k(c)], eng="act")
        for c in R4:
            t, f = FB[c]
            t2, g = GB[c]
            K.tt(gB[:, c, :n].rearrange("p (c j) -> p c j", j=64), f(256, 256 + n).rearrange("p (c j) -> p c j", j=64),
                 g(256 + 63, 256 + n)[:, ::64].unsqueeze(2).broadcast_to([128, nch, 64]), ALU.mult, [t, t2], [gB.k(c)])
        for c in R4:
            t2, g = GB[c]
            K.cp(f1[c][:, 0:nch], g(256 + 63, 256 + n)[:, ::64], [t2], [f1[c]])

    def gcx(items, lg, first, n):
        XP = [mkbuf(xp[0]), mkbuf(xp[1]), mkbuf(UW, True), mkbuf(Rk, True)]
        CA = [mkbuf(cacc[0]), mkbuf(cacc[1]), mkbuf(TA, True), mkbuf(X2, True)]
        SQ = [mkbuf(sq[0]), mkbuf(sq[1]), mkbuf(osq, True), mkbuf(vnb, True)]
        G_ = [mkbuf(t, True) for t in (Rt, DTs, DTc, egB)]
        H_ = [mkbuf(Pm, True), mkbuf(PTm, True), mkbuf(XT2, True), mkbuf(f1[3])]
        WF = [waf[0], waf[1], f1[0], f1[1]]
        PSG = [GP[0], GP[1], GP[2], GP[0]]
        R4 = range(4)
        nbs = [lg * 4 + c for c in R4]
        conv_group(items, n, lconvst, nbs, "lru_conv_w", 16, "lru_conv_b", first, XP, CA)
        for c in R4:
            wf = WF[c]
            wv = wf[:, :, :] if c < 2 else wf[:, 0:256].rearrange("p (t j) -> p t j", t=2)
            K.dma("sp", wv[:, 0, :], lru_w_a[nbs[c]], [], [wf], "wafq%d" % c)
            K.dma("sp", wv[:, 1, :], lru_w_i[nbs[c]], [], [wf], "wafq%d" % c)
            K.cp(atT[:, c, :], wv[:, 0, :], [wf], [atT], eng="pool")
            K.cp(kdt[:, c, :], wv[:, 1, :], [wf], [kdt], eng="pool")
        for c in R4:
            K.cp(SQ[c][1](0, n), CA[c][1](0, n), [CA[c][0]], [SQ[c][0]])
        for c in R4:
            K.mm(PSG[c][:, 0:n], atT[:, c, :], SQ[c][1](0, n), [atT, SQ[c][0]], [PSG[c]])
            K.mm(PSG[c][:, 256:256 + n], kdt[:, c, :], SQ[c][1](0, n), [kdt, SQ[c][0]], [PSG[c]])
            gt, g = G_[c]
            K.act(g(0, n), PSG[c][:, 0:n], AF.Sigmoid, [PSG[c], pv], [gt], bias=pvc("lru_b_a", nbs[c]))
            K.act(g(256, 256 + n), PSG[c][:, 256:256 + n], AF.Sigmoid, [PSG[c], pv], [gt], bias=pvc("lru_b_i", nbs[c]))
        for c in R4:
            gt, g = G_[c]
            ht, h = H_[c]
            K.act(h(0, n), g(0, n), AF.Exp, [gt, c1], [ht], scale=c1[:, nbs[c]:nbs[c] + 1])
        for c in R4:
            ht, h = H_[c]
            K.tt(h(256, 256 + n), h(0, n), h(0, n), ALU.mult, [ht], [ht])
        for c in R4:
            ht, h = H_[c]
            K.act(h(256, 256 + n), h(256, 256 + n), AF.Sqrt, [ht], [ht], scale=-1.0, bias=1.0)
            if first:
                K.op("dve", "memset", [], [ht], h(256, 257), 1.0)
        for c in R4:
            gt, g = G_[c]
            K.tt(g(256, 256 + n), g(256, 256 + n), CA[c][1](0, n), ALU.mult, [gt, CA[c][0]], [gt])
        for c in R4:
            gt, g = G_[c]
            ht, h = H_[c]
            K.tt(h(256, 256 + n), h(256, 256 + n), g(256, 256 + n), ALU.mult, [ht, gt], [ht])
        for c in R4:
            gt, g = G_[c]
            ht, h = H_[c]
            nb_ = nbs[c]
            if first:
                K.op("dve", "memset", [], [hst], hst[:, nb_:nb_ + 1], 0.0)
            K.op("dve", "tensor_tensor_scan", [ht, hst], [gt], out=g(0, n), data0=h(0, n), data1=h(256, 256 + n),
                 initial=hst[:, nb_:nb_ + 1], op0=ALU.mult, op1=ALU.add)
        for c in R4:
            gt, g = G_[c]
            nb_ = nbs[c]
            K.cp(hst[:, nb_:nb_ + 1], g(n - 1, n), [gt], [hst])
            K.tt(moT[:, 16 + nb_, :n], g(0, n), gD[:, c, :n], ALU.mult, [gt, gD.k(c)], [moT.k(16 + nb_)])

    def gdn_chunk(h0, ci):
        c0 = ci * 128
        g4_ = gtok[:, ci, h0:h0 + 4]
        b4_ = btok[:, ci, h0:h0 + 4]
        nb4_ = nbtok[:, ci, h0:h0 + 4]
        K.tt(Rt[:], g4_.unsqueeze(2).broadcast_to([128, 4, 128]), b4(Ucaus), ALU.mult, [gtok, cs], [Rt])
        Rf = Rt[:].rearrange("p h i -> p (h i)")
        K.mm(GP[0][:, :], Lstr, Rf, [cs, Rt], [GP[0]])
        K.mm(GP[1][:, :], onesf, Rf, [cs, Rt], [GP[1]])
        K.mm(GP[2][:, 0:4], Ucaus, g4_, [cs, gtok], [GP[2]])
        K.mm(GP[2][:, 4:8], Lstr, g4_, [cs, gtok], [GP[2]])
        K.act(DTc[:].rearrange("p h i -> p (h i)"), GP[0][:, :], AF.Exp, [GP[0]], [DTc])
        K.act(egB[:].rearrange("p h i -> p (h i)"), GP[1][:, :], AF.Exp, [GP[1]], [egB])
        K.act(esm[:, 0:8], GP[2][:, 0:8], AF.Exp, [GP[2]], [esm])
        K.tt(DTs[:], DTc[:], b4(Ustr), ALU.mult, [DTc, cs], [DTs])
        K.tt(DTc[:], DTc[:], b4(Ucaus), ALU.mult, [DTc, cs], [DTc])
        for hh in range(4):
            kc_ = gB[:, hh, c0:c0 + 128]
            qc_ = gA[:, hh, c0:c0 + 128]
            K.mm(GP[0][:, hh * 128:(hh + 1) * 128], kc_, kc_, [gB.k(hh)], [GP[0]])
            K.mm(GP[1][:, hh * 128:(hh + 1) * 128], kc_, qc_, [gB.k(hh), gA.k(hh)], [GP[1]])
            K.tr(TB[:, hh * 128:(hh + 1) * 128], gC[:, hh, c0:c0 + 128], ident_bf[:], [gC.k(hh), ident_bf], [TB])
            K.tr(TB[:, 512 + hh * 128:512 + (hh + 1) * 128], kc_, ident_bf[:], [gB.k(hh), ident_bf], [TB])
        G0 = GP[0][:, :].rearrange("p (h i) -> p h i", h=4)
        G1 = GP[1][:, :].rearrange("p (h i) -> p h i", h=4)
        G2 = GP[2][:, :].rearrange("p (h i) -> p h i", h=4)
        K.tt(Pm[:], G0, DTs[:], ALU.mult, [GP[0], DTs], [Pm])
        K.tt(Pm[:], Pm[:], nb4_.unsqueeze(2).broadcast_to([128, 4, 128]), ALU.mult, [Pm, nbtok], [Pm])
        K.tt(atT[:], G1, DTc[:], ALU.mult, [GP[1], DTc], [atT])
        TBv = TB[:, 0:512].rearrange("p (h i) -> p h i", h=4)
        TBk = TB[:, 512:1024].rearrange("p (h i) -> p h i", h=4)
        K.cp(Rk[:, :, 0:128], TBv, [TB], [Rk], eng="act")
        K.tt(Rk[:, :, 128:256], TBk, esm[:, 0:4].unsqueeze(2).broadcast_to([128, 4, 128]), ALU.mult, [TB, esm], [Rk])
        K.tt(kdt[:], TBk, esm[:, 4:8].unsqueeze(2).broadcast_to([128, 4, 128]), ALU.mult, [TB, esm], [kdt])
        K.tt(qdc[:], gA[:, :, c0:c0 + 128], egB[:], ALU.mult, [gA, egB], [qdc])
        for hh in range(4):
            K.tr(GP[2][:, hh * 128:(hh + 1) * 128], Pm[:, hh, :], ident, [Pm, cs], [GP[2]])
        K.cp(PTm[:], G2, [GP[2]], [PTm], eng="act")
        K.tt(TA[:], Pm[:], b4(ident), ALU.add, [Pm, cs], [TA])
        X, XT = Pm, PTm
        Xn, XTn = X2, XT2
        for st in range(6):
            last = (st == 5)
            for hh in range(4):
                K.mm(GP[1][:, hh * 128:(hh + 1) * 128], X[:, hh, :], XT[:, hh, :], [X, XT], [GP[1]])
            if not last:
                for hh in range(4):
                    K.mm(GP[0][:, hh * 128:(hh + 1) * 128], XT[:, hh, :], X[:, hh, :], [X, XT], [GP[0]])
            K.cp(XTn[:], G1, [GP[1]], [XTn], eng="act")
            if not last:
                K.cp(Xn[:], G0, [GP[0]], [Xn])
            for hh in range(4):
                K.mm(GP[2][:, hh * 128:(hh + 1) * 128], XTn[:, hh, :], TA[:, hh, :], [XTn, TA], [GP[2]])
            K.tt(TA[:], TA[:], G2, ALU.add, [TA, GP[2]], [TA])
            X, XT, Xn, XTn = Xn, XTn, X, XT
        for hh in range(4):
            bank = GP[hh // 2]
            K.mm(bank[:, (hh % 2) * 256:(hh % 2) * 256 + 256], TA[:, hh, :], Rk[:, hh, :], [TA, Rk], [bank])
        for b2 in range(2):
            K.tt(UW[:, 2 * b2:2 * b2 + 2, :], GP[b2][:, :].rearrange("p (h c) -> p h c", h=2),
                 b4_[:, 2 * b2:2 * b2 + 2].unsqueeze(2).broadcast_to([128, 2, 256]), ALU.mult, [GP[b2], btok], [UW])
        for hh in range(4):
            K.tr(GP[2][:, hh * 128:(hh + 1) * 128], UW[:, hh, 128:256], ident, [UW, cs], [GP[2]])
        K.cp(wTb[:], G2, [GP[2]], [wTb], eng="act")
        for hh in range(4):
            h = h0 + hh
            K.mm(GP[0][:, hh * 128:(hh + 1) * 128], wTb[:, hh, :], Sgb[:, hh, :], [wTb, Sgb.k(hh)], [GP[0]])
        K.tt(vnb[:], UW[:, :, 0:128], G0, ALU.subtract, [UW, GP[0]], [vnb])
        for hh in range(4):
            h = h0 + hh
            K.mm(GP[1][:, hh * 128:(hh + 1) * 128], Sgb[:, hh, :], qdc[:, hh, :], [Sgb.k(hh), qdc], [GP[1]], start=True,
                 stop=False)
            K.mm(GP[1][:, hh * 128:(hh + 1) * 128], vnb[:, hh, :], atT[:, hh, :], [vnb, atT], [GP[1]], start=False,
                 stop=True)
            K.mm(GP[2][:, hh * 128:(hh + 1) * 128], kdt[:, hh, :], vnb[:, hh, :], [kdt, vnb], [GP[2]])
        for hh in range(4):
            h = h0 + hh
            K.stt(Sg[:, h, :], Sg[:, h, :], egB[:, hh, 127:128], GP[2][:, hh * 128:(hh + 1) * 128], ALU.mult, ALU.add,
                  [Sg.k(h), egB, GP[2]], [Sg.k(h)])
        K.cp(Sgb[:, :, :], Sg[:, h0:h0 + 4, :], [Sg.k(h0 + i) for i in range(4)], [Sgb], eng="act")
        K.act(osq[:].rearrange("p h i -> p (h i)"), GP[1][:, :], AF.Square, [GP[1]], [osq])
        K.mm(GP[0][:, :], ones_bf[:], osq[:].rearrange("p h i -> p (h i)"), [ones_bf, osq], [GP[0]])
        K.act(f1[0][:, :], GP[0][:, :], AF.Sqrt, [GP[0]], [f1[0]], scale=1.0 / 128, bias=EPS)
        K.op("dve", "reciprocal", [f1[0]], [f1[0]], out=f1[0][:, :], in_=f1[0][:, :])
        K.tt(f1[1][:, :], GP[1][:, :], f1[0][:, :], ALU.mult, [GP[1], f1[0]], [f1[1]])
        K.stt(moT[:, h0:h0 + 4, c0:c0 + 128], f1[1][:, :].rearrange("p (h i) -> p h i", h=4), pvc("gdn_norm_w"),
              gD[:, :, c0:c0 + 128], ALU.mult, ALU.mult, [f1[1], pv, gD], [moT.k(h0 + i) for i in range(4)])

    def gb_proj(n):
        def cb(c, ps, m):
            K.act(gbT[:, 1, :n], ps[0:16, :n], AF.Sigmoid, [ps], [gbT])
        proj(ab_w_in, 0, 32, 8192, 16, hT, n, cb)

        def ca(c, ps, m):
            K.act(gbT[:, 0, :n], ps[0:16, :n], AF.Exp, [ps], [gbT], bias=pv[0:16, PV["dt_bias"]:PV["dt_bias"] + 1])
            K.act(gbT[:, 0, :n], gbT[:, 0, :n], AF.Ln, [gbT], [gbT], bias=1.0)
            K.ts(gbT[:, 0, :n], gbT[:, 0, :n], nalog[:, 0:1], None, ALU.mult, [gbT, nalog], [gbT])
        proj(ab_w_in, 0, 32, 8208, 16, hT, n, ca)

    def ab_mixer_prompt(ti, n):
        first = (ti == 0)
        rmsnorm(xres, 32, n, "norm_mix", 0, hT)

        gb_proj(n)
        for ci in range(n // 128):
            K.tr(GP[0][:, 0:16], gbT[:, 0, ci * 128:(ci + 1) * 128], cs[0:16, CI:CI + 16], [gbT, cs], [GP[0]])
            K.tr(GP[0][:, 16:32], gbT[:, 1, ci * 128:(ci + 1) * 128], cs[0:16, CI:CI + 16], [gbT, cs], [GP[0]])
            K.cp(gtok[:, ci, :], GP[0][:, 0:16], [GP[0]], [gtok])
            K.cp(btok[:, ci, :], GP[0][:, 16:32], [GP[0]], [btok])
            K.ts(nbtok[:, ci, :], GP[0][:, 16:32], -1.0, None, ALU.mult, [GP[0]], [nbtok])
        par = [0]
        for hg in range(4):
            for which, dst, scale in ((0, gA, 128.0 ** -0.5), (1, gB, 1.0), (2, gC, None)):
                def cqkv(c, ps, m, which=which, dst=dst, scale=scale):
                    cidx = which * 16 + hg * 4 + c
                    a_ = conv_fm(ps, n, convst, cidx, "gdn_conv_w", 48, cidx, None, first, par[0])
                    par[0] ^= 1
                    if scale is None:
                        K.act(dst[:, c, :n], a_[:, :n], AF.Silu, [a_], [dst.k(c)])
                    else:
                        K.act(a_[:, :n], a_[:, :n], AF.Silu, [a_], [a_])
                        l2norm_to(a_, dst[:, c, :n], n, scale)
                        K.ops[-1]["W"] = _keys([dst.k(c)])
                proj(ab_w_in, 0, 32, which * 2048 + hg * 512, 512, hT, n, None,
                     gcons=lambda items, which=which, dst=dst, scale=scale: gqkv(items, which, hg, dst, scale, first, n))

            def cz(c, ps, m):
                K.act(gD[:, c, :n], ps[:, :n], AF.Silu, [ps], [gD.k(c)])
            proj(ab_w_in, 0, 32, 6144 + hg * 512, 512, hT, n, cz)
            K.cp(Sgb[:, :, :], Sg[:, hg * 4:hg * 4 + 4, :], [Sg.k(hg * 4 + i) for i in range(4)], [Sgb], eng="act")
            for ci in range(n // 128):
                gdn_chunk(hg * 4, ci)
        for lg in range(4):
            def cy(c, ps, m):
                K.act(gD[:, c, :n], ps[:, :n], AF.Gelu_apprx_tanh, [ps], [gD.k(c)])
            proj(ab_w_in, 0, 32, 10272 + lg * 512, 512, hT, n, cy)

            def cx(c, ps, m):
                nb_ = lg * 4 + c
                p2 = nb_ % 2
                a_ = conv_fm(ps, n, lconvst, nb_, "lru_conv_w", 16, nb_, pvc("lru_conv_b", nb_), first, par[0])
                par[0] ^= 1
                K.dma("sp", waf[p2][:, 0, :], lru_w_a[nb_], [], [waf[p2]], "waf%d" % p2)
                K.dma("sp", waf[p2][:, 1, :], lru_w_i[nb_], [], [waf[p2]], "waf%d" % p2)
                K.cp(wab[p2][:], waf[p2][:, 0, :], [waf[p2]], [wab[p2]], eng="pool")
                K.cp(wib[p2][:], waf[p2][:, 1, :], [waf[p2]], [wib[p2]], eng="pool")
                xb = sq[1]
                K.cp(xb[:, :n], a_[:, :n], [a_], [xb])
                K.mm(GP[0][:, :n], wab[p2][:], xb[:, :n], [wab[p2], xb], [GP[0]])
                K.mm(GP[1][:, :n], wib[p2][:], xb[:, :n], [wib[p2], xb], [GP[1]])
                ga, gi, aa, bb = f1[0], f1[1], f1[2], f1[3]
                K.act(ga[:, :n], GP[0][:, :n], AF.Sigmoid, [GP[0], pv], [ga], bias=pvc("lru_b_a", nb_))
                K.act(gi[:, :n], GP[1][:, :n], AF.Sigmoid, [GP[1], pv], [gi], bias=pvc("lru_b_i", nb_))
                K.act(aa[:, :n], ga[:, :n], AF.Exp, [ga, c1], [aa], scale=c1[:, nb_:nb_ + 1])
                K.tt(bb[:, :n], aa[:, :n], aa[:, :n], ALU.mult, [aa], [bb])
                K.act(bb[:, :n], bb[:, :n], AF.Sqrt, [bb], [bb], scale=-1.0, bias=1.0)
                if first:
                    K.op("dve", "memset", [], [bb], bb[:, 0:1], 1.0)
                K.tt(gi[:, :n], gi[:, :n], a_[:, :n], ALU.mult, [gi, a_], [gi])
                K.tt(bb[:, :n], bb[:, :n], gi[:, :n], ALU.mult, [bb, gi], [bb])
                if first:
                    K.op("dve", "memset", [], [hst], hst[:, nb_:nb_ + 1], 0.0)
                K.op("dve", "tensor_tensor_scan", [aa, bb, hst], [ga], out=ga[:, :n], data0=aa[:, :n],
                     data1=bb[:, :n], initial=hst[:, nb_:nb_ + 1], op0=ALU.mult, op1=ALU.add)
                K.cp(hst[:, nb_:nb_ + 1], ga[:, n - 1:n], [ga], [hst])
                K.tt(moT[:, 16 + nb_, :n], ga[:, :n], gD[:, c, :n], ALU.mult, [ga, gD.k(c)], [moT.k(16 + nb_)])
            proj(ab_w_in, 0, 32, 8224 + lg * 512, 512, hT, n, None, gcons=lambda items: gcx(items, lg, first, n))
        proj(ab_w_out, 0, 32, 0, 4096, moT, n, add_resid(n))

    def c_mixer_prompt(ti, n):
        first = (ti == 0)
        rmsnorm(xres, 32, n, "norm_mix", 32, hT)
        nch = n // 64
        for hg in range(8):
            def cq(c, ps, m):
                K.act(f1[c][:, :n], ps[:, :n], AF.Silu, [ps], [f1[c]])
            proj(c_w_in, 0, 32, hg * 512, 512, hT, n, cq)

            def cf(c, ps, m):
                h = hg * 4 + c
                f_, lg_, gc_ = cacc[0], cacc[1], xp[0]
                K.act(f_[:, :n], ps[:, :n], AF.Sigmoid, [ps], [f_])
                K.ts(f_[:, :n], f_[:, :n], oml[:, h:h + 1], lb[:, h:h + 1], ALU.mult, [f_, oml, lb], [f_], op1=ALU.add)
                K.act(lg_[:, :n], f_[:, :n], AF.Ln, [f_], [lg_])
                K.op("dve", "tensor_tensor_scan", [lg_, cs], [gc_], out=gc_[:, :n], data0=cs[:, CRM:CRM + n],
                     data1=lg_[:, :n], initial=0.0, op0=ALU.mult, op1=ALU.add)
                K.ts(f_[:, :n], f_[:, :n], -1.0, 1.0, ALU.mult, [f_], [f_], op1=ALU.add)
                eg_ = xp[1]
                K.act(eg_[:, :n], gc_[:, :n], AF.Exp, [gc_], [eg_])
                K.stt(gA[:, c, :n], f1[c][:, :n], 128.0 ** -0.5, eg_[:, :n], ALU.mult, ALU.mult, [f1[c], eg_],
                      [gA.k(c)])
                K.act(lg_[:, :n], gc_[:, :n], AF.Exp, [gc_], [lg_], scale=-1.0)
                K.tt(lg_[:, :n], lg_[:, :n], f_[:, :n], ALU.mult, [lg_, f_], [lg_])
                K.cp(gE[:, c, :n], lg_[:, :n], [lg_], [gE.k(c)], eng="act")
                K.tt(gB[:, c, :n].rearrange("p (c j) -> p c j", j=64), lg_[:, :n].rearrange("p (c j) -> p c j", j=64),
                     eg_[:, 63:n:64].unsqueeze(2).broadcast_to([128, nch, 64]), ALU.mult, [lg_, eg_], [gB.k(c)])
                K.cp(f1[c][:, 0:nch], eg_[:, 63:n:64], [eg_], [f1[c]])
            proj(c_w_in, 0, 32, 4096 + hg * 512, 512, hT, n, None, gcons=lambda items: gcf(items, hg, n))

            def ci_(c, ps, m):
                K.cp(gC[:, c, :n], ps[:, :n], [ps], [gC.k(c)], eng="act")
            proj(c_w_in, 0, 32, 8192 + hg * 512, 512, hT, n, ci_)

            def cgz(c, ps, m):
                K.act(gD[:, c, :n], ps[:, :n], AF.Silu, [ps], [gD.k(c)])
            proj(c_w_in, 0, 32, 12288 + hg * 512, 512, hT, n, cgz)
            nb_ = n // 128
            SETS = [dict(shb=Shb, v=atT, kd=kdt, a=qdc, po=GP[1], psu=GP[2], sq=sq[0], rs=rstd, cc=cacc[0]),
                    dict(shb=Shb2, v=wTb, kd=vnb, a=osq, po=PB[0], psu=PB[1], sq=sq[1], rs=xp[0], cc=cacc[1])]
            for c0 in (0, 2):
                pair = [(c0, SETS[0]), (c0 + 1, SETS[1])]
                for c, S_ in pair:
                    h = hg * 4 + c
                    K.cp(S_["shb"][:, :], Sh[:, h, :], [Sh.k(h)], [S_["shb"]], eng="act")
                    for bi in range(nb_):
                        K.tr(TB[:, bi * 128:(bi + 1) * 128], gC[:, c, bi * 128:(bi + 1) * 128], ident_bf[:],
                             [gC.k(c), ident_bf], [TB])
                        K.tr(TB[:, 512 + bi * 128:512 + (bi + 1) * 128], gB[:, c, bi * 128:(bi + 1) * 128],
                             ident_bf[:], [gB.k(c), ident_bf], [TB])
                    K.cp(S_["v"][:, 0:nb_, :], TB[:, 0:n].rearrange("p (b d) -> p b d", b=nb_), [TB], [S_["v"]])
                    K.cp(S_["kd"][:, 0:nb_, :], TB[:, 512:512 + n].rearrange("p (b d) -> p b d", b=nb_), [TB],
                         [S_["kd"]], eng="act")
                    for bi in range(nb_):
                        sl_ = slice(bi * 128, (bi + 1) * 128)
                        K.mm(GP[0][:, sl_], gE[:, c, sl_], gA[:, c, sl_], [gE.k(c), gA.k(c)], [GP[0]])
                    K.tt(S_["a"][:, 0:nb_, :], GP[0][:, 0:n].rearrange("p (b i) -> p b i", b=nb_),
                         hm_bf[:].unsqueeze(1).broadcast_to([128, nb_, 128]), ALU.mult, [GP[0], hm_bf], [S_["a"]])
                for cj in range(nch):
                    bi, half = cj // 2, cj % 2
                    cols = slice(cj * 64, (cj + 1) * 64)
                    pr = slice(half * 64, (half + 1) * 64)
                    for c, S_ in pair:
                        K.mm(S_["po"][:, cols], S_["shb"][:, :], gA[:, c, cols], [S_["shb"], gA.k(c)], [S_["po"]],
                             start=True, stop=False)
                        K.mm(S_["po"][:, cols], S_["v"][:, bi, :], S_["a"][:, bi, half * 64:(half + 1) * 64],
                             [S_["v"], S_["a"]], [S_["po"]], start=False, stop=True)
                        K.mm(S_["psu"][:, 0:128], S_["kd"][pr, bi, :], S_["v"][pr, bi, :], [S_["kd"], S_["v"]],
                             [S_["psu"]])
                    for c, S_ in pair:
                        h = hg * 4 + c
                        K.stt(Sh[:, h, :], Sh[:, h, :], f1[c][:, cj:cj + 1], S_["psu"][:, 0:128], ALU.mult, ALU.add,
                              [Sh.k(h), f1[c], S_["psu"]], [Sh.k(h)])
                    for c, S_ in pair:
                        h = hg * 4 + c
                        K.cp(S_["shb"][:, :], Sh[:, h, :], [Sh.k(h)], [S_["shb"]], eng="act")
                for c, S_ in pair:
                    K.act(S_["sq"][:, :n], S_["po"][:, :n], AF.Square, [S_["po"]], [S_["sq"]])
                for i_, (c, S_) in enumerate(pair):
                    K.mm(GP[0][:, i_ * 256:i_ * 256 + n], ones_bf[:], S_["sq"][:, :n], [ones_bf, S_["sq"]], [GP[0]])
                for i_, (c, S_) in enumerate(pair):
                    K.act(S_["rs"][:, :n], GP[0][:, i_ * 256:i_ * 256 + n], AF.Sqrt, [GP[0]], [S_["rs"]],
                          scale=1.0 / 128, bias=EPS)
                for c, S_ in pair:
                    K.op("dve", "reciprocal", [S_["rs"]], [S_["rs"]], out=S_["rs"][:, :n], in_=S_["rs"][:, :n])
                for c, S_ in pair:
                    K.tt(S_["cc"][:, :n], S_["po"][:, :n], S_["rs"][:, :n], ALU.mult, [S_["po"], S_["rs"]], [S_["cc"]])
                for c, S_ in pair:
                    h = hg * 4 + c
                    K.stt(moT[:, h, :n], S_["cc"][:, :n], pvc("hgrn_norm_w"), gD[:, c, :n], ALU.mult, ALU.mult,
                          [S_["cc"], pv, gD.k(c)], [moT.k(h)])
        proj(c_w_out, 0, 32, 0, 4096, moT, n, add_resid(n))

    mem_project()
    for ti in range(NT):
        n = N
        K.dma("sp", xres[:, :, :], xT[:, ti * N:(ti + 1) * N].rearrange("(kc p) t -> p kc t", p=128), [], [xres],
              "xin")
        import os
        _st = int(os.environ.get("KSTOP", "9"))
        if _st >= 1:
            ab_mixer_prompt(ti, n)
        if _st >= 2:
            mem_attend_prompt(0, n)
        if _st >= 3:
            ffn(0, n)
        if _st >= 4:
            c_mixer_prompt(ti, n)
        if _st >= 5:
            mem_attend_prompt(1, n)
        if _st >= 6:
            ffn(1, n)
        ps = GP[0]
        for kc in range(32):
            s = sq[kc % 2]
            K.act(s[:, :n], xres[:, kc, :n], AF.Square, [xres.k(kc)], [s])
            K.mm(ps[:, :n], ones_bf[:], s[:, :n], [s, ones_bf], [ps], start=(kc == 0), stop=(kc == 31))
        K.act(rstd[:, :n], ps[:, :n], AF.Sqrt, [ps], [rstd], scale=1.0 / 4096, bias=EPS)
        K.op("dve", "reciprocal", [rstd], [rstd], out=rstd[:, :n], in_=rstd[:, :n])
        for kc in range(32):
            K.stt(xres[:, kc, :n], xres[:, kc, :n], pvc("norm_final", kc), rstd[:, :n], ALU.mult, ALU.mult,
                  [xres.k(kc), rstd, pv], [xres.k(kc)])
        K.dma("sp", yT[:, ti * N:(ti + 1) * N].rearrange("(kc p) t -> p kc t", p=128), xres[:, :, :], [xres], [],
              "yout")
    K.dma("sp", ogc, convst[:].rearrange("p c t -> p (c t)"), [convst], [], "so")
    K.dma("sp", olc, lconvst[:].rearrange("p c t -> p (c t)"), [lconvst], [], "so")
    K.dma("sp", ol, hst[:], [hst], [], "so")
    K.dma("sp", og.rearrange("h k v -> k h v"), Sg[:], [Sg], [], "so")
    K.dma("sp", oh.rearrange("h k v -> k h v"), Sh[:], [Sh], [], "so")
    AX = mybir.AxisListType.X
    XR = xres
    xs = K.sb("xs", [128, 32, TS], F32, nsub=32)
    hs = K.sb("hs", [128, 32, TS], BF16)
    ms = K.sb("ms", [128, 32, TS], BF16, nsub=32)
    xn = K.sb("xn", [128, 48, TS], F32)
    lxn = K.sb("lxn", [128, 16, TS], F32)
    xres, hT, moT = xs, hs, ms
    n = TS
    XRall = [XR]
    id16 = cs[0:16, CI:CI + 16]
    ones16 = cs[0:16, CO:CO + 128]

    def flat(t):
        return t[:].rearrange("p a b -> p (a b)")

    QKV = flat(UW)
    aB = f1[1][:, 0:256].rearrange("p (h b) -> p h b", h=16)
    bB = f1[1][:, 256:512].rearrange("p (h b) -> p h b", h=16)
    OG = f1[2][:, 0:256]
    YG = f1[3][:, 0:256].rearrange("p (c b) -> p c b", c=16)
    H0 = flat(DTs)[:, 0:256].rearrange("p (c b) -> p c b", c=16)
    HN = flat(DTc)[:, 0:256].rearrange("p (c b) -> p c b", c=16)

    def pcv(c):
        return XR[:, c // 2, 128 + (c % 2) * 64:128 + (c % 2) * 64 + 48].rearrange("p (t b) -> p t b", t=3)

    def lpcv(c):
        return XR[:, 24 + c // 2, 128 + (c % 2) * 64:128 + (c % 2) * 64 + 48].rearrange("p (t b) -> p t b", t=3)

    def bcast_rows(dst_ps, src16, b):
        pass

    K.dma("sp", xs[:, :, :], xsT.rearrange("(kc p) t -> p kc t", p=128), [], [xs], "xin")
    for half in range(2):
        K.dma("sp", XR[:, 0:24, 128 + half * 64:128 + half * 64 + 48],
              sgc.rearrange("(r two p) t b -> p r two (t b)", two=2, p=128)[:, :, half, :], [], XRall, "stin")
        K.dma("sp", XR[:, 24:32, 128 + half * 64:128 + half * 64 + 48],
              slc.rearrange("(r two p) t b -> p r two (t b)", two=2, p=128)[:, :, half, :], [], XRall, "stin")
    K.dma("sp", H0, sl.rearrange("(c p) b -> p c b", p=128), [], [DTs], "stin2")
    K.dma("sp", osgc[:, 0:2, :], sgc[:, 1:3, :], [], [], "cpy")
    K.dma("sp", oslc[:, 0:2, :], slc[:, 1:3, :], [], [], "cpy")

    def conv_s(ps, pv_, xn_t, cidx, wname, wstride, bias_ap):
        a_ = cacc[0]
        K.cp(xn_t[:, cidx, :], ps[:, :n], [ps], [xn_t], eng="act")
        K.ts(a_[:, :n], pv_[:, 0, :], pvc(wname, cidx), None, ALU.mult, XRall + [pv], [a_])
        for tap in (1, 2):
            K.stt(a_[:, :n], pv_[:, tap, :], pvc(wname, tap * wstride + cidx), a_[:, :n], ALU.mult, ALU.add,
                  XRall + [pv, a_], [a_])
        K.stt(a_[:, :n], xn_t[:, cidx, :], pvc(wname, 3 * wstride + cidx), a_[:, :n], ALU.mult, ALU.add,
              [xn_t, pv, a_], [a_])
        if bias_ap is not None:
            K.ts(a_[:, :n], a_[:, :n], bias_ap, None, ALU.add, [a_, pv], [a_])
        return a_

    def row_bcast(dst_ps, src_tile, msk_tile, b, extraR):
        K.ts(msk_tile[0:16, :, :], src_tile[0:16, :, :], cs[0:16, CI + b:CI + b + 1], None, ALU.mult,
             [src_tile, cs] + extraR, [msk_tile])
        K.mm(dst_ps[:, :], ones16, msk_tile[0:16, :, :].rearrange("p a b -> p (a b)"), [cs, msk_tile], [dst_ps])

    def v4(ap2d):
        return ap2d.rearrange("p (h i) -> p h i", h=4)

    def ab_mixer_sample():
        rmsnorm(xres, 32, n, "norm_mix", 0, hT)
        gb_proj(n)
        bc = f1[0]
        K.tt(bc[0:16, :].rearrange("p (w h b) -> p w h b", w=2, h=16),
             id16.unsqueeze(1).unsqueeze(3).broadcast_to([16, 2, 16, 16]),
             gbT[:, :, 0:16].unsqueeze(2).broadcast_to([16, 2, 16, 16]), ALU.mult, [cs, gbT], [bc])
        K.mm(GP[0][:, :], ones16, bc[0:16, :], [cs, bc], [GP[0]])
        K.act(f1[1][:, 0:256], GP[0][:, 0:256], AF.Exp, [GP[0]], [f1[1]])
        K.cp(f1[1][:, 256:512], GP[0][:, 256:512], [GP[0]], [f1[1]])
        for hg in range(4):
            for which, scale in ((0, 128.0 ** -0.5), (1, 1.0), (2, None)):
                def cqkv(c, ps, m, which=which, scale=scale):
                    cidx = which * 16 + hg * 4 + c
                    h = hg * 4 + c
                    a_ = conv_s(ps, pcv(cidx), xn, cidx, "gdn_conv_w", 48, None)
                    dst = QKV[:, which * 256 + h * 16:which * 256 + h * 16 + 16]
                    if scale is None:
                        K.act(dst, a_[:, :n], AF.Silu, [a_], [UW])
                    else:
                        K.act(a_[:, :n], a_[:, :n], AF.Silu, [a_], [a_])
                        l2norm_to(a_, dst, n, scale)
                        K.ops[-1]["W"] = _keys([UW])
                proj(ab_w_in, 0, 32, which * 2048 + hg * 512, 512, hT, n, cqkv)

            def cz(c, ps, m):
                h = hg * 4 + c
                K.act(QKV[:, 768 + h * 16:768 + h * 16 + 16], ps[:, :n], AF.Silu, [ps], [UW])
            proj(ab_w_in, 0, 32, 6144 + hg * 512, 512, hT, n, cz)
        K.dma("sp", osgc.rearrange("(c p) t b -> p c t b", p=128)[:, :, 2, :], xn[:], [xn], [], "so2")
        for hg in range(4):
            h0 = hg * 4
            for hh in range(4):
                h = h0 + hh
                K.tr(GP[2][0:16, hh * 128:(hh + 1) * 128], QKV[:, 256 + h * 16:256 + h * 16 + 16], ident, [UW, cs],
                     [GP[2]])
                K.tr(GP[0][0:16, hh * 128:(hh + 1) * 128], QKV[:, h * 16:h * 16 + 16], ident, [UW, cs], [GP[0]])
            K.cp(Pm[0:16, :, :], v4(GP[2][0:16, :]), [GP[2]], [Pm])
            K.cp(PTm[0:16, :, :], v4(GP[0][0:16, :]), [GP[0]], [PTm])
            for b in range(TS):
                bufk = [XR.k(i) for i in (range(0, 4) if b % 2 == 0 else range(4, 8))]
                r0 = 0 if b % 2 == 0 else 4
                ST4 = XR[:, r0:r0 + 4, 0:128]
                K.dma("sp", ST4, sg[b, h0:h0 + 4].rearrange("h v k -> v h k"), [], bufk, "stg%d" % (b % 2))
                row_bcast(PB[0], Pm, X2, b, [])
                row_bcast(PB[1], PTm, XT2, b, [])
                kBv, qBv = v4(PB[0][:, :]), v4(PB[1][:, :])
                a4 = aB[:, h0:h0 + 4, b]
                be4 = bB[:, h0:h0 + 4, b]
                vcol = QKV[:, 512:768].rearrange("p (h b) -> p h b", h=16)[:, h0:h0 + 4, b]
                K.tt(TA[:], ST4, kBv, ALU.mult, bufk + [PB[0]], [TA])
                K.op("dve", "reduce_sum", [TA], [esm], out=esm[:, 0:4], in_=TA[:], axis=AX)
                K.tt(esm[:, 0:4], esm[:, 0:4], a4, ALU.mult, [esm, f1[1]], [esm])
                K.tt(esm[:, 4:8], vcol, esm[:, 0:4], ALU.subtract, [UW, esm], [esm])
                K.tt(esm[:, 4:8], esm[:, 4:8], be4, ALU.mult, [esm, f1[1]], [esm])
                K.tt(TA[:], kBv, esm[:, 4:8].unsqueeze(2).broadcast_to([128, 4, 128]), ALU.mult, [PB[0], esm], [TA])
                K.tt(ST4, ST4, a4.unsqueeze(2).broadcast_to([128, 4, 128]), ALU.mult, bufk + [f1[1]], bufk)
                K.tt(ST4, ST4, TA[:], ALU.add, bufk + [TA], bufk)
                K.tt(TA[:], ST4, qBv, ALU.mult, bufk + [PB[1]], [TA])
                K.op("dve", "reduce_sum", [TA], [f1[2]], out=OG.rearrange("p (h b) -> p h b", h=16)[:, h0:h0 + 4, b],
                     in_=TA[:], axis=AX)
                K.dma("sp", osg[b, h0:h0 + 4].rearrange("h v k -> v h k"), ST4, bufk, [], "sto%d" % (b % 2))
        K.act(sq[0][:, 0:256], OG, AF.Square, [f1[2]], [sq[0]])
        K.mm(GP[0][:, 0:256], ones_bf[:], sq[0][:, 0:256], [ones_bf, sq[0]], [GP[0]])
        K.act(rstd[:, 0:256], GP[0][:, 0:256], AF.Sqrt, [GP[0]], [rstd], scale=1.0 / 128, bias=EPS)
        K.op("dve", "reciprocal", [rstd], [rstd], out=rstd[:, 0:256], in_=rstd[:, 0:256])
        K.tt(cacc[1][:, 0:256], OG, rstd[:, 0:256], ALU.mult, [f1[2], rstd], [cacc[1]])
        K.stt(moT[:, 0:16, :], cacc[1][:, 0:256].rearrange("p (h b) -> p h b", h=16), pvc("gdn_norm_w"),
              QKV[:, 768:1024].rearrange("p (h b) -> p h b", h=16), ALU.mult, ALU.mult, [cacc[1], pv, UW],
              [moT.k(i) for i in range(16)])
        for lg in range(4):
            def cy(c, ps, m):
                K.act(YG[:, lg * 4 + c, :], ps[:, :n], AF.Gelu_apprx_tanh, [ps], [f1[3]])
            proj(ab_w_in, 0, 32, 10272 + lg * 512, 512, hT, n, cy)

            def cx(c, ps, m):
                nb_ = lg * 4 + c
                p2 = nb_ % 2
                a_ = conv_s(ps, lpcv(nb_), lxn, nb_, "lru_conv_w", 16, pvc("lru_conv_b", nb_))
                K.dma("sp", waf[p2][:, 0, :], lru_w_a[nb_], [], [waf[p2]], "waf%d" % p2)
                K.dma("sp", waf[p2][:, 1, :], lru_w_i[nb_], [], [waf[p2]], "waf%d" % p2)
                K.cp(wab[p2][:], waf[p2][:, 0, :], [waf[p2]], [wab[p2]], eng="pool")
                K.cp(wib[p2][:], waf[p2][:, 1, :], [waf[p2]], [wib[p2]], eng="pool")
                xb = sq[1]
                K.cp(xb[:, :n], a_[:, :n], [a_], [xb])
                K.mm(GP[0][:, :n], wab[p2][:], xb[:, :n], [wab[p2], xb], [GP[0]])
                K.mm(GP[1][:, :n], wib[p2][:], xb[:, :n], [wib[p2], xb], [GP[1]])
                ga, gi, aa, bb = egB, Rt, X2, XT2
                gaf, gif, aaf, bbf = flat(ga), flat(gi), flat(aa), flat(bb)
                K.act(gaf[:, :n], GP[0][:, :n], AF.Sigmoid, [GP[0], pv], [ga], bias=pvc("lru_b_a", nb_))
                K.act(gif[:, :n], GP[1][:, :n], AF.Sigmoid, [GP[1], pv], [gi], bias=pvc("lru_b_i", nb_))
                K.act(aaf[:, :n], gaf[:, :n], AF.Exp, [ga, c1], [aa], scale=c1[:, nb_:nb_ + 1])
                K.tt(bbf[:, :n], aaf[:, :n], aaf[:, :n], ALU.mult, [aa], [bb])
                K.act(bbf[:, :n], bbf[:, :n], AF.Sqrt, [bb], [bb], scale=-1.0, bias=1.0)
                K.tt(gif[:, :n], gif[:, :n], a_[:, :n], ALU.mult, [gi, a_], [gi])
                K.tt(bbf[:, :n], bbf[:, :n], gif[:, :n], ALU.mult, [bb, gi], [bb])
                K.tt(gaf[:, :n], aaf[:, :n], H0[:, nb_, :], ALU.mult, [aa, DTs], [ga])
                K.tt(HN[:, nb_, :], gaf[:, :n], bbf[:, :n], ALU.add, [ga, bb], [DTc])
                K.tt(moT[:, 16 + nb_, :n], HN[:, nb_, :], YG[:, nb_, :], ALU.mult, [DTc, f1[3]], [moT.k(16 + nb_)])
            proj(ab_w_in, 0, 32, 8224 + lg * 512, 512, hT, n, cx)
        K.dma("sp", oslc.rearrange("(c p) t b -> p c t b", p=128)[:, :, 2, :], lxn[:], [lxn], [], "so2")
        K.dma("sp", osl, HN, [DTc], [], "so2")
        proj(ab_w_out, 0, 32, 0, 4096, moT, n, add_resid(n))

    def mem_attend_sample(l):
        rmsnorm(xres, 32, n, "norm_mem", 32 * l, hT)
        QS = f1[0][:, 0:64]

        def cq(c, ps, m):
            K.ts(QS[:, c * 16:(c + 1) * 16], ps[:, :n], 128.0 ** -0.5, None, ALU.mult, [ps], [f1[0]])
        proj(mem_w_q[l], 0, 32, 0, 512, hT, n, cq)
        for c in range(4):
            K.tr(GP[2][0:16, c * 128:(c + 1) * 128], QS[:, c * 16:(c + 1) * 16], ident, [f1[0], cs], [GP[2]])
        K.cp(Pm[0:16, :, :], v4(GP[2][0:16, :]), [GP[2]], [Pm])
        for b in range(TS):
            Kt = flat(UW).rearrange("p (mc x) -> p mc x", mc=2)
            Vt = flat(Rk).rearrange("p (mc x) -> p mc x", mc=2)
            K.dma("sp", Kt, cmk[l, b].rearrange("(mc p) x -> p mc x", p=128), [], [UW], "cmk")
            K.dma("sp", Vt, cmv[l, b].rearrange("(mc p) x -> p mc x", p=128), [], [Rk], "cmv")
            row_bcast(PB[0], Pm, PTm, b, [])
            sc = esm
            for mc in range(2):
                K.tt(flat(TA), Kt[:, mc, :], PB[0][:, :], ALU.mult, [UW, PB[0]], [TA])
                K.op("dve", "reduce_sum", [TA], [esm], out=esm[:, mc * 4:(mc + 1) * 4], in_=TA[:], axis=AX)
            K.act(esm[:, 0:8], esm[:, 0:8], AF.Exp, [esm], [esm])
            for mc in range(2):
                K.mm(GP[0][:, 0:4], onesf, esm[:, mc * 4:(mc + 1) * 4], [cs, esm], [GP[0]], start=(mc == 0),
                     stop=(mc == 1))
            K.op("dve", "reciprocal", [GP[0]], [c1s], out=c1s[:, 0:4], in_=GP[0][:, 0:4])
            K.tt(esm[:, 0:8].rearrange("p (mc h) -> p mc h", mc=2), esm[:, 0:8].rearrange("p (mc h) -> p mc h", mc=2),
                 c1s[:, 0:4].unsqueeze(1).broadcast_to([128, 2, 4]), ALU.mult, [esm, c1s], [esm])
            for h in range(4):
                for mc in range(2):
                    K.mm(GP[1][:, h * 16 + b:h * 16 + b + 1], Vt[:, mc, h * 128:(h + 1) * 128],
                         esm[:, mc * 4 + h:mc * 4 + h + 1], [Rk, esm], [GP[1]], start=(mc == 0), stop=(mc == 1))
        K.cp(gC[:, :, 0:16], GP[1][:, 0:64].rearrange("p (h b) -> p h b", h=4), [GP[1]], [gC])
        proj(mem_w_o[l], 0, 4, 0, 4096, gC, n, add_resid(n))

    def c_mixer_sample():
        rmsnorm(xres, 32, n, "norm_mix", 32, hT)
        Qf = f1[0].t[:, :].rearrange("p (h b) -> p h b", h=32)
        Ff = f1[1].t[:, :].rearrange("p (h b) -> p h b", h=32)
        Vf = f1[2].t[:, :].rearrange("p (h b) -> p h b", h=32)
        Gf = f1[3].t[:, :].rearrange("p (h b) -> p h b", h=32)
        OH = flat(egB)
        for hg in range(8):
            def cq(c, ps, m):
                h = hg * 4 + c
                K.act(Qf[:, h, :], ps[:, :n], AF.Silu, [ps], [f1[0]])
                K.ts(Qf[:, h, :], Qf[:, h, :], 128.0 ** -0.5, None, ALU.mult, [f1[0]], [f1[0]])
            proj(c_w_in, 0, 32, hg * 512, 512, hT, n, cq)

            def cf(c, ps, m):
                h = hg * 4 + c
                K.act(Ff[:, h, :], ps[:, :n], AF.Sigmoid, [ps], [f1[1]])
                K.ts(Ff[:, h, :], Ff[:, h, :], oml[:, h:h + 1], lb[:, h:h + 1], ALU.mult, [f1[1], oml, lb], [f1[1]],
                     op1=ALU.add)
            proj(c_w_in, 0, 32, 4096 + hg * 512, 512, hT, n, cf)

            def ci_(c, ps, m):
                K.cp(Vf[:, hg * 4 + c, :], ps[:, :n], [ps], [f1[2]], eng="act")
            proj(c_w_in, 0, 32, 8192 + hg * 512, 512, hT, n, ci_)

            def cgz(c, ps, m):
                K.act(Gf[:, hg * 4 + c, :], ps[:, :n], AF.Silu, [ps], [f1[3]])
            proj(c_w_in, 0, 32, 12288 + hg * 512, 512, hT, n, cgz)
            h0 = hg * 4
            for hh in range(4):
                K.tr(GP[2][0:16, hh * 128:(hh + 1) * 128], Ff[:, h0 + hh, :], ident, [f1[1], cs], [GP[2]])
                K.tr(GP[0][0:16, hh * 128:(hh + 1) * 128], Qf[:, h0 + hh, :], ident, [f1[0], cs], [GP[0]])
            K.cp(Pm[0:16, :, :], v4(GP[2][0:16, :]), [GP[2]], [Pm])
            K.cp(PTm[0:16, :, :], v4(GP[0][0:16, :]), [GP[0]], [PTm])
            for b in range(TS):
                bufk = [XR.k(i) for i in (range(0, 4) if b % 2 == 0 else range(4, 8))]
                r0 = 0 if b % 2 == 0 else 4
                ST4 = XR[:, r0:r0 + 4, 0:128]
                K.dma("sp", ST4, sh[b, h0:h0 + 4].rearrange("h v k -> v h k"), [], bufk, "stg%d" % (b % 2))
                row_bcast(PB[0], Pm, X2, b, [])
                row_bcast(PB[1], PTm, XT2, b, [])
                fBv, qBv = v4(PB[0][:, :]), v4(PB[1][:, :])
                vb = Vf[:, h0:h0 + 4, b].unsqueeze(2).broadcast_to([128, 4, 128])
                K.tt(TA[:], ST4, fBv, ALU.mult, bufk + [PB[0]], [TA])
                K.tt(DTs[:], fBv, vb, ALU.mult, [PB[0], f1[2]], [DTs])
                K.tt(TA[:], TA[:], DTs[:], ALU.subtract, [TA, DTs], [TA])
                K.tt(ST4, TA[:], vb, ALU.add, [TA, f1[2]], bufk)
                K.tt(TA[:], ST4, qBv, ALU.mult, bufk + [PB[1]], [TA])
                K.op("dve", "reduce_sum", [TA], [egB], out=OH.rearrange("p (h b) -> p h b", h=32)[:, h0:h0 + 4, b],
                     in_=TA[:], axis=AX)
                K.dma("sp", osh[b, h0:h0 + 4].rearrange("h v k -> v h k"), ST4, bufk, [], "sto%d" % (b % 2))
        K.act(sq[0][:, :], OH, AF.Square, [egB], [sq[0]])
        K.mm(GP[0][:, :], ones_bf[:], sq[0][:, :], [ones_bf, sq[0]], [GP[0]])
        K.act(rstd[:, :], GP[0][:, :], AF.Sqrt, [GP[0]], [rstd], scale=1.0 / 128, bias=EPS)
        K.op("dve", "reciprocal", [rstd], [rstd], out=rstd[:, :], in_=rstd[:, :])
        K.tt(flat(TA), OH, rstd[:, :], ALU.mult, [egB, rstd], [TA])
        K.stt(moT[:, :, :], flat(TA).rearrange("p (h b) -> p h b", h=32), pvc("hgrn_norm_w"), Gf, ALU.mult, ALU.mult,
              [TA, pv, f1[3]], [moT])
        proj(c_w_out, 0, 32, 0, 4096, moT, n, add_resid(n))

    c1s = K.sb("c1s", [128, 4], F32)
    ab_mixer_sample()
    mem_attend_sample(0)
    ffn(0, n)
    c_mixer_sample()
    mem_attend_sample(1)
    ffn(1, n)
    ps = GP[0]
    for kc in range(32):
        s = sq[kc % 2]
        K.act(s[:, :n], xres[:, kc, :n], AF.Square, [xres.k(kc)], [s])
        K.mm(ps[:, :n], ones_bf[:], s[:, :n], [s, ones_bf], [ps], start=(kc == 0), stop=(kc == 31))
    K.act(rstd[:, :n], ps[:, :n], AF.Sqrt, [ps], [rstd], scale=1.0 / 4096, bias=EPS)
    K.op("dve", "reciprocal", [rstd], [rstd], out=rstd[:, :n], in_=rstd[:, :n])
    for kc in range(32):
        K.stt(xres[:, kc, :n], xres[:, kc, :n], pvc("norm_final", kc), rstd[:, :n], ALU.mult, ALU.mult,
              [xres.k(kc), rstd, pv], [xres.k(kc)])
    K.dma("sp", ysT.rearrange("(kc p) t -> p kc t", p=128), xres[:, :, :], [xres], [], "yout")
    return K.finalize()


_NC_CACHE = {}


def _fm(v):
    v = np.asarray(v, np.float32)
    return np.ascontiguousarray(v.reshape(-1, 128).T)


def kernel(**inp):
    NT = 2048 // NTILE
    if NT not in _NC_CACHE:
        _NC_CACHE[NT] = build(NT)
    nc = _NC_CACHE[NT]
    f = lambda a: np.ascontiguousarray(np.asarray(a, np.float32))
    pvh = np.zeros((128, PVN), np.float32)

    def put(name, cols):
        pvh[:, PV[name]:PV[name] + cols.shape[1]] = cols
    put("norm_mix", _fm(inp["norm_mix"])); put("norm_mem", _fm(inp["norm_mem"]))
    put("norm_mem_kv", _fm(inp["norm_mem_kv"])); put("norm_ffn", _fm(inp["norm_ffn"]))
    put("norm_final", _fm(inp["norm_final"])); put("gdn_conv_w", _fm(inp["gdn_conv_w"][0]))
    put("lru_conv_w", _fm(inp["lru_conv_w"][0])); put("lru_conv_b", _fm(inp["lru_conv_b"][0]))
    put("lru_b_a", _fm(inp["lru_b_a"][0])); put("lru_b_i", _fm(inp["lru_b_i"][0]))
    put("lru_lam", _fm(inp["lru_lam"][0])); put("lb_raw", _fm(inp["hgrn_lb_raw"]))
    put("gdn_norm_w", _fm(inp["gdn_norm_w"][0])); put("hgrn_norm_w", _fm(inp["hgrn_norm_w"][0]))
    pvh[0:16, PV["a_log"]] = np.asarray(inp["gdn_a_log"][0], np.float32)
    pvh[0:16, PV["dt_bias"]] = np.asarray(inp["gdn_dt_bias"][0], np.float32)
    consts = make_consts()
    shared = dict(pv=pvh, consts=consts, ab_w_in=f(inp["ab_w_in"][0]), ab_w_out=f(inp["ab_w_out"][0]),
                  c_w_in=f(inp["c_w_in"][0]), c_w_out=f(inp["c_w_out"][0]), mem_w_q=f(inp["mem_w_q"]),
                  mem_w_k=f(inp["mem_w_k"]), mem_w_v=f(inp["mem_w_v"]), mem_w_o=f(inp["mem_w_o"]),
                  ffn_w_up=f(inp["ffn_w_up"]), ffn_w_down=f(inp["ffn_w_down"]), lru_w_a=f(inp["lru_w_a"][0]),
                  lru_w_i=f(inp["lru_w_i"][0]))
    in_maps = []
    for c in range(NCORES):
        s = c % 4
        b0 = c * TS
        m = dict(shared)
        m["xT"] = f(np.asarray(inp["x_prompt"][s]).T)
        m["memT"] = f(np.asarray(inp["mem_prompt"][s]).T)
        m["xsT"] = f(np.asarray(inp["x_sample"][b0:b0 + TS, 0, :]).T)
        m["cmk"] = f(np.asarray(inp["cache_mem_k"][:, b0:b0 + TS]).reshape(2, TS, 256, 512))
        m["cmv"] = f(np.asarray(inp["cache_mem_v"][:, b0:b0 + TS]).reshape(2, TS, 256, 512))
        m["sgc"] = f(np.asarray(inp["state_gdn_conv"][0, b0:b0 + TS]).transpose(2, 1, 0))
        m["sg"] = f(np.asarray(inp["state_gdn"][0, b0:b0 + TS]).transpose(0, 1, 3, 2))
        m["slc"] = f(np.asarray(inp["state_lru_conv"][0, b0:b0 + TS]).transpose(2, 1, 0))
        m["sl"] = f(np.asarray(inp["state_lru"][0, b0:b0 + TS]).T)
        m["sh"] = f(np.asarray(inp["state_hgrn"][0, b0:b0 + TS]).transpose(0, 1, 3, 2))
        in_maps.append(m)
    res = run_bass_kernel_spmd(nc, in_maps, core_ids=list(range(NCORES)))
    R = res.results
    y_prompt = np.stack([R[s]["yT"].T for s in range(4)])
    y_sample = np.concatenate([R[c]["ysT"].T for c in range(NCORES)])[:, None, :]
    pk = np.stack([np.stack([R[s]["okT"][l].T.reshape(256, 4, 128) for s in range(4)]) for l in range(2)])
    pvv = np.stack([np.stack([R[s]["ov"][l].reshape(256, 4, 128) for s in range(4)]) for l in range(2)])
    p_gc = np.stack([R[s]["ogc"].reshape(128, 48, 3).transpose(2, 1, 0).reshape(3, 6144) for s in range(4)])[None]
    p_g = np.stack([R[s]["og"] for s in range(4)])[None]
    p_lc = np.stack([R[s]["olc"].reshape(128, 16, 3).transpose(2, 1, 0).reshape(3, 2048) for s in range(4)])[None]
    p_l = np.stack([R[s]["ol"].T.reshape(2048) for s in range(4)])[None]
    p_h = np.stack([R[s]["oh"] for s in range(4)])[None]
    s_gc = np.concatenate([R[c]["osgc"].transpose(2, 1, 0) for c in range(NCORES)])[None]
    s_g = np.concatenate([R[c]["osg"].transpose(0, 1, 3, 2) for c in range(NCORES)])[None]
    s_lc = np.concatenate([R[c]["oslc"].transpose(2, 1, 0) for c in range(NCORES)])[None]
    s_l = np.concatenate([R[c]["osl"].transpose(2, 1, 0).reshape(TS, 2048) for c in range(NCORES)])[None]
    s_h = np.concatenate([R[c]["osh"].transpose(0, 1, 3, 2) for c in range(NCORES)])[None]
    outs = (y_prompt, y_sample, pk, pvv, p_gc, p_g, p_lc, p_l, p_h, s_gc, s_g, s_lc, s_l, s_h)
    return tuple(np.ascontiguousarray(o, dtype=np.float32) for o in outs)
```
